# Optimizing a Trainium2 kernel written in Bass

```python
import jax, jax.numpy as jnp
from jax import lax
import numpy as np

D_MODEL = 2048
BATCH = 16
SEQ = 2048
DEPTH = 1
DEC_BATCH = 16
DEC_SEQ = 16
PAST_LEN = 1024

CHUNK = 64
N_HEADS = 8
HEAD_DIM = 128
ATT_W = N_HEADS * HEAD_DIM
D_CONV = D_MODEL // 2
CONV_W = 31
FFN_CONV_W = 3
D_FF = 5632
Q_BLOCK = 128
EPS = 1e-6
NEG_INF = -1e30
N_IN = 2 * D_CONV + 3 * ATT_W + N_HEADS + 2 * D_MODEL

kernel_name = "hybrid_conformer_fox_streaming_step"


def rmsnorm(x, g):
    xf = x.astype(jnp.float32)
    r = xf * lax.rsqrt(jnp.mean(xf * xf, axis=-1, keepdims=True) + EPS)
    return (r * g.astype(jnp.float32)).astype(x.dtype)


def layernorm(x, g, b):
    xf = x.astype(jnp.float32)
    mu = jnp.mean(xf, axis=-1, keepdims=True)
    xc = xf - mu
    r = xc * lax.rsqrt(jnp.mean(xc * xc, axis=-1, keepdims=True) + EPS)
    return (r * g.astype(jnp.float32) + b.astype(jnp.float32)).astype(x.dtype)


def causal_dwconv(x_ext, w, b):
    c = x_ext.shape[-1]
    y = lax.conv_general_dilated(
        x_ext, w[:, None, :].astype(x_ext.dtype), window_strides=(1,), padding="VALID",
        dimension_numbers=("NWC", "WIO", "NWC"), feature_group_count=c)
    return y + b.astype(y.dtype)


def fox_probs(s, bias, mask):
    return jax.nn.softmax(jnp.where(mask[None, None], s + bias, NEG_INF), axis=-1)


def fox_attention_prompt(q, k, v, logf):
    b, s, h, d = q.shape
    nb = s // Q_BLOCK
    scale = d ** -0.5
    c_t = jnp.transpose(jnp.cumsum(logf, axis=1), (0, 2, 1))
    qb = jnp.transpose(q.reshape(b, nb, Q_BLOCK, h, d), (1, 0, 2, 3, 4))
    cb = jnp.transpose(c_t.reshape(b, h, nb, Q_BLOCK), (2, 0, 1, 3))
    key_pos = jnp.arange(s)

    def block(args):
        i, qi, ci = args
        sc = jnp.einsum("bqhd,bkhd->bhqk", qi, k, preferred_element_type=jnp.float32) * scale
        bias = ci[..., :, None] - c_t[:, :, None, :]
        q_pos = i * Q_BLOCK + jnp.arange(Q_BLOCK)
        p = fox_probs(sc, bias, key_pos[None, :] <= q_pos[:, None])
        return jnp.einsum("bhqk,bkhd->bqhd", p.astype(v.dtype), v)

    o = lax.map(block, (jnp.arange(nb, dtype=jnp.int32), qb, cb))
    return jnp.transpose(o, (1, 0, 2, 3, 4)).reshape(b, s, h * d)


def fox_attention_sample(q, k, v, logf, past_k, past_v, past_logf):
    b, t, h, d = q.shape
    p_len = past_k.shape[1]
    scale = d ** -0.5
    kk = jnp.concatenate([past_k.astype(k.dtype), k], axis=1)
    vv = jnp.concatenate([past_v.astype(v.dtype), v], axis=1)
    lf = jnp.concatenate([past_logf.astype(jnp.float32), logf], axis=1)
    c_t = jnp.transpose(jnp.cumsum(lf, axis=1), (0, 2, 1))
    sc = jnp.einsum("bqhd,bkhd->bhqk", q, kk, preferred_element_type=jnp.float32) * scale
    bias = c_t[:, :, p_len:, None] - c_t[:, :, None, :]
    mask = jnp.arange(p_len + t)[None, :] <= (p_len + jnp.arange(t))[:, None]
    p = fox_probs(sc, bias, mask)
    o = jnp.einsum("bhqk,bkhd->bqhd", p.astype(vv.dtype), vv)
    return o.reshape(b, t, h * d)


def hybrid_layer(x, conv_buf, ffn_buf, attn_fn,
                 norm_mix_g, w_in, b_forget, conv_dw_w, conv_dw_b, conv_ln_g, conv_ln_b,
                 conv_pw_out, q_norm_g, k_norm_g, w_att_out, w_out,
                 norm_ffn_g, w_up, ffn_dw_w, ffn_dw_b, w_down):
    b, t, _ = x.shape
    h = rmsnorm(x, norm_mix_g)
    proj = h @ w_in
    o0 = 2 * D_CONV
    o1 = o0 + ATT_W
    o2 = o1 + ATT_W
    o3 = o2 + ATT_W
    o4 = o3 + N_HEADS
    o5 = o4 + D_MODEL
    glu_in = proj[..., :o0]
    q = proj[..., o0:o1].reshape(b, t, N_HEADS, HEAD_DIM)
    k = proj[..., o1:o2].reshape(b, t, N_HEADS, HEAD_DIM)
    v = proj[..., o2:o3].reshape(b, t, N_HEADS, HEAD_DIM)
    f_logit = proj[..., o3:o4]
    gate_a = jax.nn.sigmoid(proj[..., o4:o5])
    gate_b = jax.nn.sigmoid(proj[..., o5:])

    u = glu_in[..., :D_CONV] * jax.nn.sigmoid(glu_in[..., D_CONV:])
    ext = jnp.concatenate([conv_buf.astype(u.dtype), u], axis=1)
    z = layernorm(causal_dwconv(ext, conv_dw_w, conv_dw_b), conv_ln_g, conv_ln_b)
    a_out = (z * jax.nn.sigmoid(z)) @ conv_pw_out
    new_conv = ext[:, -(CONV_W - 1):]

    q = rmsnorm(q, q_norm_g)
    k = rmsnorm(k, k_norm_g)
    logf = jax.nn.log_sigmoid(f_logit.astype(jnp.float32) + b_forget.astype(jnp.float32))
    b_out = attn_fn(q, k, v, logf) @ w_att_out

    x = x + (gate_a * a_out + gate_b * b_out) @ w_out

    h2 = rmsnorm(x, norm_ffn_g)
    up = h2 @ w_up
    g_half = up[..., :D_FF]
    u_half = up[..., D_FF:]
    ext_f = jnp.concatenate([ffn_buf.astype(g_half.dtype), g_half], axis=1)
    gc = causal_dwconv(ext_f, ffn_dw_w, ffn_dw_b)
    x = x + (jax.nn.silu(gc) * u_half) @ w_down
    new_ffn = ext_f[:, -(FFN_CONV_W - 1):]
    return x, new_conv, new_ffn, k, v, logf


def setup_inputs(seed: int = 0) -> dict:
    key = jax.random.key(seed)
    ks = jax.random.split(key, 32)
    f32 = jnp.float32
    nrm = lambda k, shape, s: (jax.random.normal(k, shape, f32) * s)
    b_forget = (jnp.linspace(1.0, 6.0, N_HEADS, dtype=f32)[None, :]
                + nrm(ks[7], (DEPTH, N_HEADS), 0.1))
    return {
        "x_prompt": nrm(ks[0], (BATCH, SEQ, D_MODEL), 1.0),
        "x_sample": nrm(ks[1], (DEC_BATCH, DEC_SEQ, D_MODEL), 1.0),
        "cache_k": nrm(ks[2], (DEPTH, DEC_BATCH, PAST_LEN, N_HEADS, HEAD_DIM), 1.0),
        "cache_v": nrm(ks[3], (DEPTH, DEC_BATCH, PAST_LEN, N_HEADS, HEAD_DIM), 1.0),
        "cache_logf": jax.nn.log_sigmoid(b_forget[:, None, None, :]
                                          + nrm(ks[4], (DEPTH, DEC_BATCH, PAST_LEN, N_HEADS), 1.0)),
        "state_conv": nrm(ks[5], (DEPTH, DEC_BATCH, CONV_W - 1, D_CONV), 0.5),
        "state_ffn": nrm(ks[6], (DEPTH, DEC_BATCH, FFN_CONV_W - 1, D_FF), 1.0),
        "norm_mix_g": 1.0 + nrm(ks[8], (DEPTH, D_MODEL), 0.05),
        "w_in": nrm(ks[9], (DEPTH, D_MODEL, N_IN), D_MODEL ** -0.5),
        "b_forget": b_forget,
        "conv_dw_w": nrm(ks[10], (DEPTH, CONV_W, D_CONV), CONV_W ** -0.5),
        "conv_dw_b": nrm(ks[11], (DEPTH, D_CONV), 0.02),
        "conv_ln_g": 1.0 + nrm(ks[12], (DEPTH, D_CONV), 0.05),
        "conv_ln_b": nrm(ks[13], (DEPTH, D_CONV), 0.02),
        "conv_pw_out": nrm(ks[14], (DEPTH, D_CONV, D_MODEL), D_CONV ** -0.5),
        "q_norm_g": 1.0 + nrm(ks[15], (DEPTH, HEAD_DIM), 0.05),
        "k_norm_g": 1.0 + nrm(ks[16], (DEPTH, HEAD_DIM), 0.05),
        "w_att_out": nrm(ks[17], (DEPTH, ATT_W, D_MODEL), ATT_W ** -0.5),
        "w_out": nrm(ks[18], (DEPTH, D_MODEL, D_MODEL), D_MODEL ** -0.5),
        "norm_ffn_g": 1.0 + nrm(ks[19], (DEPTH, D_MODEL), 0.05),
        "w_up": nrm(ks[20], (DEPTH, D_MODEL, 2 * D_FF), D_MODEL ** -0.5),
        "ffn_dw_w": nrm(ks[21], (DEPTH, FFN_CONV_W, D_FF), FFN_CONV_W ** -0.5),
        "ffn_dw_b": nrm(ks[22], (DEPTH, D_FF), 0.02),
        "w_down": nrm(ks[23], (DEPTH, D_FF, D_MODEL), D_FF ** -0.5),
    }


def reference(x_prompt, x_sample, cache_k, cache_v, cache_logf, state_conv, state_ffn,
              norm_mix_g, w_in, b_forget, conv_dw_w, conv_dw_b, conv_ln_g, conv_ln_b,
              conv_pw_out, q_norm_g, k_norm_g, w_att_out, w_out,
              norm_ffn_g, w_up, ffn_dw_w, ffn_dw_b, w_down):
    bp = x_prompt.shape[0]
    xp = x_prompt
    xs = x_sample
    pk, pv, plf, pconv, pffn = [], [], [], [], []
    sk, sv, slf, sconv, sffn = [], [], [], [], []
    for l in range(DEPTH):
        weights = (norm_mix_g[l], w_in[l], b_forget[l], conv_dw_w[l], conv_dw_b[l],
                   conv_ln_g[l], conv_ln_b[l], conv_pw_out[l], q_norm_g[l], k_norm_g[l],
                   w_att_out[l], w_out[l], norm_ffn_g[l], w_up[l], ffn_dw_w[l], ffn_dw_b[l],
                   w_down[l])
        zero_conv = jnp.zeros((bp, CONV_W - 1, D_CONV), xp.dtype)
        zero_ffn = jnp.zeros((bp, FFN_CONV_W - 1, D_FF), xp.dtype)
        xp, c_p, f_p, k_p, v_p, lf_p = hybrid_layer(
            xp, zero_conv, zero_ffn, fox_attention_prompt, *weights)
        ck, cv, clf = cache_k[l], cache_v[l], cache_logf[l]
        attn_s = lambda q, k, v, lf, ck=ck, cv=cv, clf=clf: fox_attention_sample(q, k, v, lf, ck, cv, clf)
        xs, c_s, f_s, k_s, v_s, lf_s = hybrid_layer(
            xs, state_conv[l], state_ffn[l], attn_s, *weights)
        pk.append(k_p); pv.append(v_p); plf.append(lf_p); pconv.append(c_p); pffn.append(f_p)
        sk.append(k_s); sv.append(v_s); slf.append(lf_s); sconv.append(c_s); sffn.append(f_s)
    return (xp, xs,
            jnp.stack(pk), jnp.stack(pv), jnp.stack(plf), jnp.stack(pconv), jnp.stack(pffn),
            jnp.stack(sk), jnp.stack(sv), jnp.stack(slf), jnp.stack(sconv), jnp.stack(sffn))
```

```python
import numpy as np
import os
import types
from contextlib import ExitStack
import concourse.bass as bass
import concourse.mybir as mybir
from concourse.bass_utils import run_bass_kernel_spmd

F32 = mybir.dt.float32
BF16 = mybir.dt.bfloat16
AF = mybir.ActivationFunctionType
ALU = mybir.AluOpType
AX = mybir.AxisListType
ENGS = ['pe', 'act', 'dve', 'pool', 'sp']

D = 2048
SEQ = 2048
NH = 8
HD = 128
AW = 1024
DC = 1024
CW = 31
DFF = 5632
NF = DFF // 128
NIN = 2 * DC + 3 * AW + NH + 2 * D
O_Q = 2 * DC
O_K = O_Q + AW
O_V = O_K + AW
O_F = O_V + AW
O_GA = O_F + NH
O_GB = O_GA + D
EPS = 1e-6
NEG = -30000.0
PAST = 1024
DSEQ = 16
KBMAX = 17


def _freeze(fn):
    if fn.__closure__ is None:
        return fn
    cells = []
    for c in fn.__closure__:
        try:
            cells.append(types.CellType(c.cell_contents))
        except ValueError:
            cells.append(c)
    g = types.FunctionType(fn.__code__, fn.__globals__, fn.__name__, fn.__defaults__, tuple(cells))
    g.__kwdefaults__ = fn.__kwdefaults__
    return g


class Sched:
    R = int(os.environ.get("DBG_R", "8"))

    def __init__(self):
        self.ops = []
        self.cells = {}
        self.eng_ops = {e: [] for e in ENGS}
        self.known = {e: {f: -1 for f in ENGS} for e in ENGS}
        self.known_dma = {e: set() for e in ENGS}
        self.dma_q = {e: [] for e in ENGS}

    def add(self, eng, fn, reads=(), writes=(), dma=False, extra_deps=()):
        idx = len(self.ops)
        op = dict(idx=idx, eng=eng, fn=_freeze(fn), dma=dma, seq=len(self.eng_ops[eng]),
                  deps_c={}, deps_d=[], flagged=False, val=None, slot=None)
        deps = set()
        for c in reads:
            st = self.cells.get(c)
            if st is None:
                st = self.cells[c] = [None, {}, []]
            if st[0] is not None:
                deps.add(st[0])
        for c in writes:
            st = self.cells.get(c)
            if st is None:
                st = self.cells[c] = [None, {}, []]
            if st[0] is not None:
                deps.add(st[0])
            deps.update(st[1].values())
            deps.update(st[2])
            st[0] = idx
            st[1] = {}
            st[2] = []
        wset = set(writes)
        for c in reads:
            if c in wset:
                continue
            st = self.cells[c]
            if dma:
                st[2].append(idx)
            else:
                st[1][eng] = idx
        if dma:
            q = self.dma_q[eng]
            if len(q) >= self.R:
                deps.add(q[len(q) - self.R])
            q.append(idx)
        deps.update(extra_deps)
        deps.discard(idx)
        kn = self.known[eng]
        for d in sorted(deps):
            od = self.ops[d]
            if od['dma']:
                if d in self.known_dma[eng]:
                    continue
                self.known_dma[eng].add(d)
                op['deps_d'].append(d)
            else:
                f = od['eng']
                if f == 'pe' and eng == 'pe':
                    continue
                if kn[f] >= od['seq']:
                    continue
                if op['deps_c'].get(f, -1) < od['seq']:
                    op['deps_c'][f] = od['seq']
        for f, s in op['deps_c'].items():
            kn[f] = s
            self.ops[self.eng_ops[f][s]]['flagged'] = True
        self.ops.append(op)
        self.eng_ops[eng].append(idx)
        return idx

    def barrier(self):
        last = [self.eng_ops[e][-1] for e in ENGS if self.eng_ops[e]]
        last = [i for i in last if not self.ops[i]['dma']]
        for e in ENGS:
            comp = [i for i in reversed(self.eng_ops[e]) if not self.ops[i]['dma']]
            if comp and comp[0] not in last:
                last.append(comp[0])
        dmas = [i for e in ENGS for i in self.dma_q[e][-self.R:]]
        for e in os.environ.get('DBG_BAR_ENGS', 'act').split(','):
            self.add(e, lambda eng: eng.nop(), extra_deps=last + dmas)

    def emit(self, nc, block, csem, dsem):
        R = self.R
        for e in ENGS:
            cnt = 0
            for idx in self.eng_ops[e]:
                op = self.ops[idx]
                if op['dma']:
                    continue
                if op['flagged']:
                    cnt += 1
                    op['val'] = cnt
            for i, idx in enumerate(self.dma_q[e]):
                op = self.ops[idx]
                op['slot'] = i % R
                op['val'] = 16 * (i // R + 1)

        def run(e, eng):
            for idx in self.eng_ops[e]:
                op = self.ops[idx]
                for f, s in op['deps_c'].items():
                    od = self.ops[self.eng_ops[f][s]]
                    eng.wait_ge(csem[f], od['val'])
                for d in op['deps_d']:
                    od = self.ops[d]
                    eng.wait_ge(dsem[od['eng']][od['slot']], od['val'])
                ins = op['fn'](eng)
                if op['dma']:
                    ins.then_inc(dsem[e][op['slot']], 16)
                elif op['flagged']:
                    ins.then_inc(csem[e], 1)
            if e == 'sp':
                for q in ENGS:
                    for idx in self.dma_q[q][-R:]:
                        od = self.ops[idx]
                        eng.wait_ge(dsem[q][od['slot']], od['val'])
                for f in ENGS:
                    if f == 'sp':
                        continue
                    last = None
                    for idx in self.eng_ops[f]:
                        if self.ops[idx]['val'] is not None and not self.ops[idx]['dma']:
                            last = self.ops[idx]
                    if last is not None:
                        eng.wait_ge(csem[f], last['val'])

        @block.tensor
        def _(eng):
            run('pe', eng)

        @block.scalar
        def _(eng):
            run('act', eng)

        @block.vector
        def _(eng):
            run('dve', eng)

        @block.gpsimd
        def _(eng):
            run('pool', eng)

        @block.sync
        def _(eng):
            run('sp', eng)


C_ID, C_TRI, C_TRIS, C_E0, C_EA, C_EB, C_ONES, C_ONESM, C_MASK, C_MASKS = 0, 128, 256, 384, 512, 640, 768, 896, 1024, 1920
NCON = 1984
P_CW, P_CB, P_LG, P_LB, P_FW, P_FB, P_G1, P_G2, P_QG, P_KG, P_BF = 0, 248, 256, 264, 272, 404, 448, 464, 480, 608, 736
NPAR = 744


def make_consts():
    c = np.zeros((128, NCON), np.float32)
    p = np.arange(128)
    c[:, C_ID:C_ID + 128] = np.eye(128)
    c[:, C_TRI:C_TRI + 128] = (p[:, None] <= p[None, :])
    seg = np.full(128, -1)
    seg[0:16] = 0
    seg[32:48] = 1
    same = (seg[:, None] == seg[None, :]) & (seg[:, None] >= 0)
    c[:, C_TRIS:C_TRIS + 128] = same & (p[:, None] <= p[None, :])
    c[0, C_E0:C_E0 + 128] = 1.0
    c[0, C_EA:C_EA + 16] = 1.0
    c[0, C_EB + 32:C_EB + 48] = 1.0
    c[:, C_ONES:C_ONES + 128] = 1.0
    c[:, C_ONESM:C_ONESM + 128] = 1.0 / DC
    j = np.arange(896)
    c[:, C_MASK:C_MASK + 896] = np.where(p[:, None] <= j[None, :] - 384, 0.0, NEG)
    c[:, C_MASKS:C_MASKS + 64] = np.where(same[:, 0:64] & (p[:, None] <= p[None, 0:64]), 0.0, NEG)
    return c


def make_params(norm_mix_g, b_forget, conv_dw_w, conv_dw_b, conv_ln_g, conv_ln_b, q_norm_g, k_norm_g,
                norm_ffn_g, ffn_dw_w, ffn_dw_b):
    P = np.zeros((128, NPAR), np.float32)
    P[:, P_CW:P_CW + 248] = conv_dw_w.reshape(CW, 8, 128).transpose(2, 1, 0).reshape(128, 248)
    P[:, P_CB:P_CB + 8] = conv_dw_b.reshape(8, 128).T
    P[:, P_LG:P_LG + 8] = conv_ln_g.reshape(8, 128).T
    P[:, P_LB:P_LB + 8] = conv_ln_b.reshape(8, 128).T
    P[:, P_FW:P_FW + 132] = ffn_dw_w.reshape(3, NF, 128).transpose(2, 1, 0).reshape(128, 132)
    P[:, P_FB:P_FB + NF] = ffn_dw_b.reshape(NF, 128).T
    P[:, P_G1:P_G1 + 16] = norm_mix_g.reshape(16, 128).T
    P[:, P_G2:P_G2 + 16] = norm_ffn_g.reshape(16, 128).T
    P[:, P_QG:P_QG + 128] = q_norm_g.reshape(1, 128)
    P[:, P_KG:P_KG + 128] = k_norm_g.reshape(1, 128)
    P[:, P_BF:P_BF + 8] = b_forget.reshape(1, 8)
    return P


def build_program(tile_sel=None, upto=99, conv_w=True):
    nc = bass.Bass("TRN2", target_bir_lowering=False)
    S = Sched()

    def din(name, shape, dt=F32):
        return nc.dram_tensor(name, list(shape), dt, kind="ExternalInput").ap()

    def dout(name, shape):
        return nc.dram_tensor(name, list(shape), F32, kind="ExternalOutput").ap()

    def dscr(name, shape, dt):
        return nc.dram_tensor(name, list(shape), dt, kind="Internal").ap()

    xp = din("xp", [2, SEQ, D])
    xs = din("xs", [2, DSEQ, D])
    ck = din("ck", [2, PAST, AW])
    cv = din("cv", [2, PAST, AW])
    clf = din("clf", [2, PAST, NH])
    stc = din("stc", [2, CW - 1, DC])
    stf = din("stf", [2, 2, DFF])
    wf = {"in": din("w_in", [D, NIN]), "pw": din("w_pw", [DC, D]), "ao": din("w_ao", [AW, D]),
          "o": din("w_o", [D, D]), "up": din("w_up", [D, 2 * DFF]), "dn": din("w_dn", [DFF, D])}
    params = din("params", [128, NPAR])
    consts = din("consts", [128, NCON])
    yp = dout("yp", [2, SEQ, D])
    ys = dout("ys", [2, DSEQ, D])
    pk = dout("pk", [2, SEQ, AW])
    pv = dout("pv", [2, SEQ, AW])
    plf = dout("plf", [2, SEQ, NH])
    pconv = dout("pconv", [2, CW - 1, DC])
    pffn = dout("pffn", [2, 2, DFF])
    sk = dout("sk", [2, DSEQ, AW])
    sv = dout("sv", [2, DSEQ, AW])
    slf = dout("slf", [2, DSEQ, NH])
    sconv = dout("sconv", [2, CW - 1, DC])
    sffn = dout("sffn", [2, 2, DFF])
    wb = {k: dscr("wb_" + k, v.shape, BF16) for k, v in wf.items()}
    dbg = nc.dram_tensor("dbg", [6, 128, 8192], BF16, kind="ExternalOutput").ap() if os.environ.get("DBG_WB") else None
    dbg2 = nc.dram_tensor("dbg2", [128, 4 * KBMAX * 8 + 32], F32, kind="ExternalOutput").ap() if os.environ.get("DBG2") else None
    KTd = dscr("KTd", [4, NH, 128, KBMAX * 128], BF16)
    Vd = dscr("Vd", [4, NH, 128, KBMAX, 128], BF16)

    XIN = [xp[0], xp[1], xs[0], xs[1]]
    YOUT = [yp[0], yp[1], ys[0], ys[1]]
    KOUT = [pk[0], pk[1], sk[0], sk[1]]
    VOUT = [pv[0], pv[1], sv[0], sv[1]]
    LFOUT = [plf[0], plf[1], slf[0], slf[1]]
    CONVOUT = [pconv[0], pconv[1], sconv[0], sconv[1]]
    FFNOUT = [pffn[0], pffn[1], sffn[0], sffn[1]]

    es = ExitStack()
    with es:
        def sb(name, shape, dt):
            return es.enter_context(nc.sbuf_tensor(name, list(shape), dt))

        def ps(name, shape, dt):
            return es.enter_context(nc.psum_tensor(name, list(shape), dt))

        CON = sb("CON", [128, NCON], F32)
        PAR = sb("PAR", [128, NPAR], F32)
        IDB = sb("IDB", [128, 128], BF16)
        ONB = sb("ONB", [128, 128], BF16)
        X = sb("X", [128, 4, D], F32)
        HT = sb("HT", [128, 16, 512], BF16)
        HB = sb("HB", [128, D], BF16)
        NWB = 3
        WB = [sb("WB%d" % i, [128, 8192], BF16) for i in range(NWB)]
        SS = sb("SS", [128, 8], F32)
        RS = sb("RS", [128, 8], F32)
        JUNK = sb("JUNK", [128, D], BF16)
        UH = sb("UH", [128, 8, 30], F32)
        UHS = sb("UHS", [128, 2, 8, 30], F32)
        GH = sb("GH", [128, 4, NF, 2], F32)
        NCK = sb("NCK", [128, 4, KBMAX, 8], F32)
        CAR = sb("CAR", [128, 4, 8], F32)
        LF = sb("LF", [128, 4, 8], F32)
        CC = sb("CC", [128, 4, 8], F32)
        SM = sb("SM", [128, 64], F32)
        DG = [sb("DG%d" % i, [128, 128], F32) for i in range(2)]
        TT = [sb("TT%d" % i, [128, 512], F32) for i in range(2)]
        PP = [sb("PP%d" % i, [128, 512], BF16) for i in range(4)]
        RD = sb("RD", [128, 512], F32)
        JF = JUNK[:].bitcast(F32)
        TTV = [TT[0][:, :], TT[1][:, :], JF[:, 0:512], JF[:, 512:1024]]
        TTC = [[("TT", 0)], [("TT", 1)], ["JUNK"], ["JUNK2"]]
        T1 = [sb("T1_%d" % i, [128, 512], F32) for i in range(2)]
        T2 = [sb("T2_%d" % i, [128, 512], F32) for i in range(2)]
        KF = [sb("KF%d" % i, [128, 512], F32) for i in range(2)]
        SQ = sb("SQ", [128, 512], F32)
        ARENA_KB = 60
        ARf = sb("ARf", [128, ARENA_KB * 256], F32)
        ARb = ARf[:].bitcast(BF16)
        PSB = [ps("PS%d" % i, [128, 512], F32) for i in range(6)]
        PTB = [ps("PT%d" % i, [128, 1024], BF16) for i in range(2)]
        csem = {e: es.enter_context(nc.semaphore("c_" + e)) for e in ENGS}
        dsem = {e: [es.enter_context(nc.semaphore("d_%s%d" % (e, i))) for i in range(Sched.R)] for e in ENGS}
        block = es.enter_context(nc.Block())

        _A = S.add
        NOFP32 = os.environ.get("DBG_NOFP32")

        def A(eng, fn, reads=(), writes=(), dma=False, fp32pe=False):
            if fp32pe and NOFP32:
                return None
            return _A(eng, fn, reads, writes, dma)
        con = lambda off, n=128, rows=128: CON[0:rows, off:off + n]

        def arc(off_b, nbytes):
            return [("AR", b) for b in range(off_b // 256, (off_b + nbytes + 255) // 256)]

        def arf(off_b, n):
            return ARf[:, off_b // 4: off_b // 4 + n]

        def arb(off_b, n):
            return ARb[:, off_b // 2: off_b // 2 + n]

        KB = 1024
        U_OFF, U_STRIDE = 0, 576 * 4
        Z_OFF = 18 * KB
        ZB_OFF = 34 * KB
        LN_OFF = 42 * KB
        ST_OFF = 0
        QT_OFF = 24 * KB
        CQ_OFF = 42 * KB
        ATT_OFF = 0
        KV_OFF = 8 * KB
        MIX_OFF = 8 * KB
        ACT_OFF = 0
        G_OFF = 44 * KB
        FT_OFF = 53 * KB
        KST_OFF = 58 * KB

        XBv = X[:].bitcast(BF16)

        def gate_ap(which, fo, T):
            c = which * 2 + fo // 8
            return XBv[:, c, (fo % 8) * 512:(fo % 8) * 512 + T], [("X", c)]

        psn = [0]

        def bank(excl=()):
            while True:
                b = psn[0] % 6
                psn[0] += 1
                if ("PS", b) not in excl:
                    return PSB[b], ("PS", b)

        ptn = [0]

        def ptbank():
            b = ptn[0] % 2
            ptn[0] += 1
            return PTB[b], ("PT", b)

        wbn = [0]

        def wbuf():
            b = wbn[0] % NWB
            wbn[0] += 1
            return WB[b], ("WB", b)

        A('sp', lambda e: e.dma_start(out=CON[:], in_=consts), writes=["CON"], dma=True)
        A('sp', lambda e: e.dma_start(out=PAR[:], in_=params), writes=["PAR"], dma=True)
        A('dve', lambda e: e.tensor_copy(out=IDB[:], in_=con(C_ID)), reads=["CON"], writes=["IDB"])
        A('dve', lambda e: e.tensor_copy(out=ONB[:], in_=con(C_ONES)), reads=["CON"], writes=["ONB"])
        A('pool', lambda e: e.memset(X[:], 0.0), writes=[("X", c) for c in range(4)])
        A('pool', lambda e: e.memset(GH[:], 0.0), writes=[("GH", q) for q in range(4)])
        A('pool', lambda e: e.memset(CAR[:], 0.0), writes=[("CAR", q) for q in range(4)])
        A('pool', lambda e: e.memset(ARf[:], 0.0), writes=arc(0, ARENA_KB * KB))

        def wconv(name):
            Kr, Nc = wf[name].shape
            for r0 in range(0, Kr, 256):
                r1 = min(Kr, r0 + 256)
                A('pool', lambda e, r0=r0, r1=r1: e.dma_start(out=wb[name][r0:r1, :], in_=wf[name][r0:r1, :]),
                  writes=[("wb", name, r0 // 256)], dma=True)

        def wcells(name, c0=0, c1=0):
            Kr = wf[name].shape[0]
            return [("wb", name, r) for r in range((Kr + 255) // 256)]

        def wload_cols(name, nk, c0, ncols, buf=None, boff=0):
            if buf is None:
                buf = wbuf()
            Wt, wc = buf
            src = wb[name].rearrange("(k p) n -> p k n", p=128)[:, 0:nk, c0:c0 + ncols]
            dst = Wt[:, boff:boff + nk * ncols].rearrange("p (k n) -> p k n", k=nk)
            A('sp', lambda e: e.dma_start(out=dst, in_=src), reads=wcells(name, c0, c0 + ncols), writes=[wc], dma=True)
            return buf

        def norm_transpose(tile, gcol):
            for ci, ch in enumerate(tile['chunks']):
                rows = ch['rows']
                A('pool', lambda e, ci=ci: e.memset(SS[:, ci:ci + 1], 0.0), writes=[("SS", ci)])
                A('act', lambda e, ci=ci, rows=rows: e.activation(out=JUNK[0:rows, :], in_=X[0:rows, ci, :], func=AF.Square,
                                                                  accum_out=SS[0:rows, ci:ci + 1]),
                  reads=[("X", ci), ("SS", ci)], writes=["JUNK", "JUNK2", ("SS", ci)])
                A('act', lambda e, ci=ci: e.activation(out=RS[:, ci:ci + 1], in_=SS[:, ci:ci + 1], func=AF.Ln, bias=EPS, scale=1.0 / D),
                  reads=[("SS", ci)], writes=[("RS", ci)])
                A('act', lambda e, ci=ci: e.activation(out=RS[:, ci:ci + 1], in_=RS[:, ci:ci + 1], func=AF.Exp, scale=-0.5),
                  reads=[("RS", ci)], writes=[("RS", ci)])
                A('dve', lambda e, ci=ci, rows=rows: e.tensor_scalar(out=HB[0:rows, :], in0=X[0:rows, ci, :], scalar1=RS[0:rows, ci:ci + 1],
                                                                     scalar2=None, op0=ALU.mult),
                  reads=[("X", ci), ("RS", ci)], writes=["HB"])
                for g8 in range(2):
                    pt, ptc = ptbank()
                    for k in range(8):
                        fi = g8 * 8 + k
                        A('pe', lambda e, pt=pt, k=k, fi=fi, rows=rows: e.transpose(pt[:, k * 128:k * 128 + rows], HB[0:rows, fi * 128:(fi + 1) * 128],
                                                                                    IDB[0:rows, 0:rows]),
                          reads=["HB", "IDB"], writes=[ptc])
                    for k in range(8):
                        fi = g8 * 8 + k
                        eng = 'dve'
                        if eng == 'act':
                            fn = lambda e, pt=pt, k=k, fi=fi, rows=rows, ci=ci: e.activation(
                                out=HT[:, fi, ci * 128:ci * 128 + rows], in_=pt[:, k * 128:k * 128 + rows], func=AF.Copy,
                                scale=PAR[:, gcol + fi:gcol + fi + 1])
                        else:
                            fn = lambda e, pt=pt, k=k, fi=fi, rows=rows, ci=ci: e.tensor_scalar(
                                out=HT[:, fi, ci * 128:ci * 128 + rows], in0=pt[:, k * 128:k * 128 + rows],
                                scalar1=PAR[:, gcol + fi:gcol + fi + 1], scalar2=None, op0=ALU.mult)
                        A(eng, fn, reads=[ptc, "PAR"], writes=[("HT", fi)])

        def load_x(tile):
            for ci, ch in enumerate(tile['chunks']):
                for (p0, n, q, t0) in ch['rg']:
                    A('sp', lambda e, ci=ci, p0=p0, n=n, q=q, t0=t0: e.dma_start(out=X[p0:p0 + n, ci, :], in_=XIN[q][t0:t0 + n, :]),
                      writes=[("X", ci)], dma=True)

        def ws_group(name, nk, c0, ngrp, rhs_fn, rhs_cells, T, buf=None, boff=0):
            buf = wload_cols(name, nk, c0, ngrp * 128, buf, boff)
            Wt, wc = buf
            outs = []
            for g in range(ngrp):
                pb, pc = bank()
                for k in range(nk):
                    A('pe', lambda e, pb=pb, k=k, g=g: e.matmul(pb[:, 0:T], lhsT=Wt[:, boff + k * ngrp * 128 + g * 128: boff + k * ngrp * 128 + (g + 1) * 128],
                                                                  rhs=rhs_fn(k), start=(k == 0), stop=(k == nk - 1)),
                      reads=[wc] + rhs_cells(k), writes=[pc])
                outs.append((pb, pc))
            return outs

        def useg(si, c):
            off = U_OFF + c * U_STRIDE if si == 0 else U_OFF + c * U_STRIDE + 64 * 4
            return off

        def phase_glu_conv(tile):
            T = tile['T']
            segs = tile['segs']
            for si, sg in enumerate(segs):
                q = sg['q']
                for c in range(8):
                    off = useg(si, c)
                    if sg['kind'] == 'prompt':
                        if sg['first']:
                            A('pool', lambda e, off=off: e.memset(arf(off, 30), 0.0), writes=arc(off, 120))
                        else:
                            A('pool', lambda e, off=off, c=c: e.tensor_copy(out=arf(off, 30), in_=UH[:, c, :]), reads=[("UH", c)], writes=arc(off, 120))
                    else:
                        A('pool', lambda e, off=off, c=c, q=q: e.tensor_copy(out=arf(off, 30), in_=UHS[:, q - 2, c, :]), reads=[("UHS", q - 2, c)], writes=arc(off, 120))
            hrhs = lambda k: HT[:, k, 0:T]
            hcells = lambda k: [("HT", k)]
            for c2 in range(4):
                buf = wbuf()
                wload_cols("in", 16, c2 * 256, 256, buf, 0)
                wload_cols("in", 16, DC + c2 * 256, 256, buf, 4096)
                Wt, wc = buf
                pbs = []
                for half in range(2):
                    for g in range(2):
                        pb, pc = bank()
                        boff = half * 4096
                        for k in range(16):
                            A('pe', lambda e, pb=pb, k=k, g=g, boff=boff: e.matmul(pb[:, 0:T], lhsT=Wt[:, boff + k * 256 + g * 128: boff + k * 256 + (g + 1) * 128],
                                                                                     rhs=hrhs(k), start=(k == 0), stop=(k == 15)),
                              reads=[wc] + hcells(k), writes=[pc])
                        pbs.append((pb, pc))
                for g in range(2):
                    c = c2 * 2 + g
                    (pa, pac), (pbb, pbc) = pbs[g], pbs[2 + g]
                    sgt = T1[g]
                    A('act', lambda e, sgt=sgt, pbb=pbb: e.activation(out=sgt[:, 0:T], in_=pbb[:, 0:T], func=AF.Sigmoid), reads=[pbc], writes=[("T1", g)])
                    for si, sg in enumerate(segs):
                        off = useg(si, c) + 120
                        c0, n = sg['col0'], sg['n']
                        A('dve', lambda e, off=off, pa=pa, sgt=sgt, c0=c0, n=n: e.tensor_tensor(out=arf(off, n), in0=pa[:, c0:c0 + n], in1=sgt[:, c0:c0 + n], op=ALU.mult),
                          reads=[pac, ("T1", g)], writes=arc(off, n * 4))
        def phase_gate_proj(tile):
            T = tile['T']
            for g2 in range(8):
                ba = wbuf()
                wload_cols("in", 16, O_GA + g2 * 256, 256, ba, 0)
                wload_cols("in", 16, O_GB + g2 * 256, 256, ba, 4096)
                Wt, wc = ba
                for g in range(2):
                    fo = g2 * 2 + g
                    for which, boff in ((0, 0), (1, 4096)):
                        pb, pc = bank()
                        for k in range(16):
                            A('pe', lambda e, pb=pb, Wt=Wt, k=k, g=g, boff=boff: e.matmul(pb[:, 0:T], lhsT=Wt[:, boff + k * 256 + g * 128: boff + k * 256 + (g + 1) * 128],
                                                                                          rhs=HT[:, k, 0:T], start=(k == 0), stop=(k == 15)),
                              reads=[wc, ("HT", k)], writes=[pc])
                        gap, gcells = gate_ap(which, fo, T)
                        A('act', lambda e, gap=gap, pb=pb: e.activation(out=gap, in_=pb[:, 0:T], func=AF.Sigmoid), reads=[pc], writes=gcells)

        def phase_conv_ln(tile):
            T = tile['T']
            segs = tile['segs']
            for si, sg in enumerate(segs):
                q, n, c0 = sg['q'], sg['n'], sg['col0']
                for c in range(8):
                    off = useg(si, c)
                    if sg['last']:
                        if c % 4 == 0:
                            cvb, cvc = bank()
                        A('pe', lambda e, off=off, c=c, n=n, cvb=cvb: e.transpose(cvb[0:30, (c % 4) * 128:(c % 4 + 1) * 128], arf(off + n * 4, 30), con(C_ID)),
                          reads=arc(off + n * 4, 120) + ["CON"], writes=[cvc])
                        if c % 4 == 3:
                            lo = 46 * KB + (c // 4) * 2048
                            A('act', lambda e, cvb=cvb, lo=lo: e.activation(out=arf(lo, 512)[0:30, :], in_=cvb[0:30, :], func=AF.Copy), reads=[cvc], writes=arc(lo, 2048))
                            A('sp', lambda e, lo=lo, q=q, c=c: e.dma_start(out=CONVOUT[q][:, (c // 4) * 512:(c // 4 + 1) * 512], in_=arf(lo, 512)[0:30, :]), reads=arc(lo, 2048), dma=True)
                    elif sg['kind'] == 'prompt':
                        A('pool', lambda e, off=off, c=c, n=n: e.tensor_copy(out=UH[:, c, :], in_=arf(off + n * 4, 30)), reads=arc(off + n * 4, 120), writes=[("UH", c)])
                    zoff = Z_OFF + c * 2048 + c0 * 4
                    A('dve', lambda e, off=off, zoff=zoff, c=c, n=n: e.tensor_scalar(out=arf(zoff, n), in0=arf(off, n), scalar1=PAR[:, P_CW + c * 31:P_CW + c * 31 + 1],
                                                                                      scalar2=PAR[:, P_CB + c:P_CB + c + 1], op0=ALU.mult, op1=ALU.add),
                      reads=arc(off, n * 4) + ["PAR"], writes=arc(zoff, n * 4))
            for tap in range(1, CW):
                for si, sg in enumerate(segs):
                    n, c0 = sg['n'], sg['col0']
                    for c in range(8):
                        off = useg(si, c)
                        zoff = Z_OFF + c * 2048 + c0 * 4
                        A('dve', lambda e, off=off, zoff=zoff, c=c, n=n, tap=tap: e.scalar_tensor_tensor(
                            out=arf(zoff, n), in0=arf(off + tap * 4, n), scalar=PAR[:, P_CW + c * 31 + tap:P_CW + c * 31 + tap + 1], in1=arf(zoff, n),
                            op0=ALU.mult, op1=ALU.add),
                          reads=arc(off + tap * 4, n * 4) + arc(zoff, n * 4), writes=arc(zoff, n * 4))
            pm, pmc = bank()
            pq, pqc = bank()
            for c in range(8):
                zoff = Z_OFF + c * 2048
                A('pe', lambda e, zoff=zoff, c=c: e.matmul(pm[:, 0:T], lhsT=con(C_ONESM), rhs=arf(zoff, T), start=(c == 0), stop=(c == 7)),
                  reads=["CON"] + arc(zoff, T * 4), writes=[pmc], fp32pe=True)
            for c in range(8):
                zoff = Z_OFF + c * 2048
                A('act', lambda e, zoff=zoff: e.activation(out=SQ[:, 0:T], in_=arf(zoff, T), func=AF.Square), reads=arc(zoff, T * 4), writes=["SQ"])
                A('pe', lambda e, c=c: e.matmul(pq[:, 0:T], lhsT=con(C_ONESM), rhs=SQ[:, 0:T], start=(c == 0), stop=(c == 7)),
                  reads=["CON", "SQ"], writes=[pqc], fp32pe=True)
            MEAN, RSTD = LN_OFF, LN_OFF + 2048
            A('act', lambda e: e.activation(out=arf(MEAN, T), in_=pm[:, 0:T], func=AF.Copy), reads=[pmc], writes=arc(MEAN, T * 4))
            A('dve', lambda e: e.tensor_tensor(out=arf(RSTD, T), in0=arf(MEAN, T), in1=arf(MEAN, T), op=ALU.mult), reads=arc(MEAN, T * 4), writes=arc(RSTD, T * 4))
            A('dve', lambda e: e.tensor_tensor(out=arf(RSTD, T), in0=pq[:, 0:T], in1=arf(RSTD, T), op=ALU.subtract), reads=[pqc] + arc(RSTD, T * 4), writes=arc(RSTD, T * 4))
            A('act', lambda e: e.activation(out=arf(RSTD, T), in_=arf(RSTD, T), func=AF.Ln, bias=EPS, scale=1.0), reads=arc(RSTD, T * 4), writes=arc(RSTD, T * 4))
            A('act', lambda e: e.activation(out=arf(RSTD, T), in_=arf(RSTD, T), func=AF.Exp, scale=-0.5), reads=arc(RSTD, T * 4), writes=arc(RSTD, T * 4))
            for c in range(8):
                zoff = Z_OFF + c * 2048
                zb = ZB_OFF + c * 1024
                t1 = T1[c % 2]
                t2 = T2[c % 2]
                A('dve', lambda e, zoff=zoff, t1=t1: e.tensor_tensor(out=t1[:, 0:T], in0=arf(zoff, T), in1=arf(MEAN, T), op=ALU.subtract),
                  reads=arc(zoff, T * 4) + arc(MEAN, T * 4), writes=[("T1", c % 2)])
                A('pool', lambda e, t1=t1: e.tensor_tensor(out=t1[:, 0:T], in0=t1[:, 0:T], in1=arf(RSTD, T), op=ALU.mult),
                  reads=[("T1", c % 2)] + arc(RSTD, T * 4), writes=[("T1", c % 2)])
                A('pool', lambda e, t1=t1, c=c: e.tensor_scalar(out=t1[:, 0:T], in0=t1[:, 0:T], scalar1=PAR[:, P_LG + c:P_LG + c + 1], scalar2=PAR[:, P_LB + c:P_LB + c + 1],
                                                                op0=ALU.mult, op1=ALU.add), reads=[("T1", c % 2), "PAR"], writes=[("T1", c % 2)])
                A('act', lambda e, t1=t1, t2=t2: e.activation(out=t2[:, 0:T], in_=t1[:, 0:T], func=AF.Sigmoid), reads=[("T1", c % 2)], writes=[("T2", c % 2)])
                A('dve', lambda e, t1=t1, t2=t2, zb=zb: e.tensor_tensor(out=arb(zb, T), in0=t1[:, 0:T], in1=t2[:, 0:T], op=ALU.mult),
                  reads=[("T1", c % 2), ("T2", c % 2)], writes=arc(zb, T * 2))

        def cumsum_chunk(rows, lf_ap, lf_cells, tri_off, carries, c_out, c_cells, nck_out, nck_cells, car_upd):
            pb, pc = bank()
            n_mm = 1 + len(carries)
            A('pe', lambda e: e.matmul(pb[0:rows, 0:8], lhsT=CON[0:rows, tri_off:tri_off + rows], rhs=lf_ap, start=True, stop=(n_mm == 1)),
              reads=["CON"] + lf_cells, writes=[pc], fp32pe=True)
            for i, (sel_off, car_ap, car_cells) in enumerate(carries):
                A('pe', lambda e, sel_off=sel_off, car_ap=car_ap, i=i: e.matmul(pb[0:rows, 0:8], lhsT=CON[:, sel_off:sel_off + rows], rhs=car_ap, start=False, stop=(i == n_mm - 2)),
                  reads=["CON"] + car_cells, writes=[pc], fp32pe=True)
            A('act', lambda e: e.activation(out=c_out, in_=pb[0:rows, 0:8], func=AF.Copy), reads=[pc], writes=c_cells)
            A('dve', lambda e: e.tensor_scalar(out=nck_out, in0=pb[0:rows, 0:8], scalar1=-1.0, scalar2=None, op0=ALU.mult), reads=[pc], writes=nck_cells)
            if car_upd is not None:
                car_ap, car_cells = car_upd
                pb2, pc2 = bank()
                A('pe', lambda e: e.matmul(pb2[:, 0:8], lhsT=CON[0:rows, C_ONES:C_ONES + 128], rhs=lf_ap, start=True, stop=False),
                  reads=["CON"] + lf_cells, writes=[pc2], fp32pe=True)
                A('pe', lambda e: e.matmul(pb2[:, 0:8], lhsT=con(C_E0), rhs=car_ap, start=False, stop=True), reads=["CON"] + car_cells, writes=[pc2], fp32pe=True)
                A('act', lambda e: e.activation(out=car_ap, in_=pb2[:, 0:8], func=AF.Copy), reads=[pc2], writes=car_cells)

        def kt_ingest(rows, kb_ap_fn, kb_cells, qs, kb):
            pt, ptc = ptbank()
            for h in range(NH):
                A('pe', lambda e, h=h: e.transpose(pt[:, h * 128:h * 128 + rows], kb_ap_fn(h), IDB[0:rows, 0:rows]), reads=kb_cells + ["IDB"], writes=[ptc])
            kst = KST_OFF
            A('act', lambda e: e.activation(out=arb(kst, 1024).rearrange("p (h t) -> p h t", h=NH)[:, :, 0:rows],
                                            in_=pt[:].rearrange("p (h t) -> p h t", h=NH)[:, :, 0:rows], func=AF.Copy),
              reads=[ptc], writes=arc(kst, 2048))
            for q in qs:
                A('sp', lambda e, q=q: e.dma_start(out=KTd[q].rearrange("h d t -> d h t")[:, :, kb * 128:kb * 128 + rows],
                                                   in_=arb(kst, 1024).rearrange("p (h t) -> p h t", h=NH)[:, :, 0:rows]),
                  reads=arc(kst, 2048), writes=[("KTd", q, kb)], dma=True)

        def phase_qkv(tile):
            T = tile['T']
            chunks = tile['chunks']
            nch = len(chunks)
            stq = lambda ci: ST_OFF + ci * 2048
            stk = lambda ci: ST_OFF + 8 * KB + ci * 2048
            stv = lambda ci: ST_OFF + 16 * KB + ci * 2048
            for ct in range(6):
                kind = ct // 2
                half = ct % 2
                buf = wload_cols("in", 16, O_Q + ct * 512, 512)
                Wt, wc = buf
                if dbg is not None:
                    A('sp', lambda e, Wt=Wt, ct=ct: e.dma_start(out=dbg[ct], in_=Wt[:, :]), reads=[wc], dma=True)
                for ci, ch in enumerate(chunks):
                    rows = ch['rows']
                    pb, pc = bank()
                    for k in range(16):
                        A('pe', lambda e, pb=pb, k=k, ci=ci, rows=rows: e.matmul(pb[0:rows, :], lhsT=HT[:, k, ci * 128:ci * 128 + rows], rhs=Wt[:, k * 512:(k + 1) * 512],
                                                                                    start=(k == 0), stop=(k == 15)),
                          reads=[wc, ("HT", k)], writes=[pc])
                    if kind < 2:
                        gcol = P_QG if kind == 0 else P_KG
                        A('act', lambda e, pb=pb, rows=rows: e.activation(out=SQ[0:rows, :], in_=pb[0:rows, :], func=AF.Square), reads=[pc], writes=["SQ"])
                        A('dve', lambda e, rows=rows: e.tensor_reduce(out=SM[0:rows, 0:4], in_=SQ[0:rows, :].rearrange("p (a b) -> p a b", a=4), axis=AX.X, op=ALU.add),
                          reads=["SQ"], writes=["SM0"])
                        A('act', lambda e, rows=rows: e.activation(out=SM[0:rows, 0:4], in_=SM[0:rows, 0:4], func=AF.Ln, bias=EPS, scale=1.0 / HD), reads=["SM0"], writes=["SM0"])
                        A('act', lambda e, rows=rows: e.activation(out=SM[0:rows, 0:4], in_=SM[0:rows, 0:4], func=AF.Exp, scale=-0.5), reads=["SM0"], writes=["SM0"])
                        kf = KF[(ct * nch + ci) % 2]
                        kfc = ("KF", (ct * nch + ci) % 2)
                        A('dve', lambda e, pb=pb, rows=rows, kf=kf: e.tensor_tensor(out=kf[0:rows, :].rearrange("p (a b) -> p a b", a=4), in0=pb[0:rows, :].rearrange("p (a b) -> p a b", a=4),
                                                                                 in1=SM[0:rows, 0:4].unsqueeze(2).to_broadcast([rows, 4, 128]), op=ALU.mult),
                          reads=[pc, "SM0"], writes=[kfc])
                        A('pool', lambda e, rows=rows, kf=kf, gcol=gcol: e.tensor_tensor(out=kf[0:rows, :].rearrange("p (a b) -> p a b", a=4), in0=kf[0:rows, :].rearrange("p (a b) -> p a b", a=4),
                                                                                      in1=PAR[0:rows, gcol:gcol + 128].unsqueeze(1).to_broadcast([rows, 4, 128]), op=ALU.mult),
                          reads=[kfc, "PAR"], writes=[kfc])
                        so = (stq(ci) if kind == 0 else stk(ci)) + half * 1024
                        A('pool', lambda e, rows=rows, kf=kf, so=so: e.tensor_copy(out=arb(so, 512)[0:rows, :], in_=kf[0:rows, :]), reads=[kfc], writes=arc(so, 1024))
                        if kind == 1:
                            for (p0, n, q, t0) in ch['rg']:
                                A('sp', lambda e, kf=kf, p0=p0, n=n, q=q, t0=t0, half=half: e.dma_start(out=KOUT[q][t0:t0 + n, half * 512:(half + 1) * 512], in_=kf[p0:p0 + n, :]),
                                  reads=[kfc], dma=True)
                    else:
                        kf = KF[(ct * nch + ci) % 2]
                        kfc = ("KF", (ct * nch + ci) % 2)
                        A('act', lambda e, pb=pb, rows=rows, kf=kf: e.activation(out=kf[0:rows, :], in_=pb[0:rows, :], func=AF.Copy), reads=[pc], writes=[kfc])
                        so = stv(ci) + half * 1024
                        A('pool', lambda e, rows=rows, kf=kf, so=so: e.tensor_copy(out=arb(so, 512)[0:rows, :], in_=kf[0:rows, :]), reads=[kfc], writes=arc(so, 1024))
                        for (p0, n, q, t0) in ch['rg']:
                            A('sp', lambda e, kf=kf, p0=p0, n=n, q=q, t0=t0, half=half: e.dma_start(out=VOUT[q][t0:t0 + n, half * 512:(half + 1) * 512], in_=kf[p0:p0 + n, :]),
                              reads=[kfc], dma=True)
            buf = wload_cols("in", 16, O_F, 8)
            Wt, wc = buf
            for ci, ch in enumerate(chunks):
                rows = ch['rows']
                pb, pc = bank()
                for k in range(16):
                    A('pe', lambda e, pb=pb, k=k, ci=ci, rows=rows: e.matmul(pb[0:rows, 0:8], lhsT=HT[:, k, ci * 128:ci * 128 + rows], rhs=Wt[:, k * 8:(k + 1) * 8],
                                                                                start=(k == 0), stop=(k == 15)),
                      reads=[wc, ("HT", k)], writes=[pc])
                t = SM[0:rows, 8:16]
                a = SM[0:rows, 16:24]
                m = SM[0:rows, 24:32]
                A('dve', lambda e, pb=pb, rows=rows, t=t: e.tensor_tensor(out=t, in0=pb[0:rows, 0:8], in1=PAR[0:rows, P_BF:P_BF + 8], op=ALU.add), reads=[pc, "PAR"], writes=["SM1"])
                A('dve', lambda e, t=t, a=a: e.tensor_scalar(out=a, in0=t, scalar1=-1.0, scalar2=None, op0=ALU.mult), reads=["SM1"], writes=["SM2"])
                A('dve', lambda e, t=t, a=a: e.tensor_tensor(out=a, in0=a, in1=t, op=ALU.max), reads=["SM1", "SM2"], writes=["SM2"])
                A('act', lambda e, a=a: e.activation(out=a, in_=a, func=AF.Exp, scale=-1.0), reads=["SM2"], writes=["SM2"])
                A('act', lambda e, a=a: e.activation(out=a, in_=a, func=AF.Ln, bias=1.0, scale=1.0), reads=["SM2"], writes=["SM2"])
                A('dve', lambda e, t=t, m=m: e.tensor_scalar_min(out=m, in0=t, scalar1=0.0), reads=["SM1"], writes=["SM3"])
                A('dve', lambda e, a=a, m=m, ci=ci, rows=rows: e.tensor_tensor(out=LF[0:rows, ci, :], in0=m, in1=a, op=ALU.subtract), reads=["SM2", "SM3"], writes=[("LF", ci)])
                for (p0, n, q, t0) in ch['rg']:
                    A('sp', lambda e, ci=ci, p0=p0, n=n, q=q, t0=t0: e.dma_start(out=LFOUT[q][t0:t0 + n, :], in_=LF[p0:p0 + n, ci, :]), reads=[("LF", ci)], dma=True)
            for ci, ch in enumerate(chunks):
                rows = ch['rows']
                pt, ptc = ptbank()
                for h in range(NH):
                    so = stq(ci) + h * 256
                    A('pe', lambda e, h=h, so=so, rows=rows, pt=pt: e.transpose(pt[:, h * 128:h * 128 + rows], arb(so, 128)[0:rows, :], IDB[0:rows, 0:rows]),
                      reads=arc(so, 256) + ["IDB"], writes=[ptc])
                A('dve', lambda e, ci=ci, rows=rows, pt=pt: e.tensor_copy(out=arb(QT_OFF, 4096).rearrange("p (h t) -> p h t", h=NH)[:, :, ci * 128:ci * 128 + rows],
                                                                        in_=pt[:].rearrange("p (h t) -> p h t", h=NH)[:, :, 0:rows]),
                  reads=[ptc], writes=arc(QT_OFF, 8192))
                kb = ch['kb']
                qs = ch['qs']
                kt_ingest(rows, lambda h, ci=ci, rows=rows: arb(stk(ci) + h * 256, 128)[0:rows, :], arc(stk(ci), 2048), qs, kb)
                for q in qs:
                    A('sp', lambda e, q=q, ci=ci, rows=rows, kb=kb: e.dma_start(out=Vd[q].rearrange("h p k d -> p h k d")[0:rows, :, kb, :],
                                                                              in_=arb(stv(ci), 1024)[0:rows, :].rearrange("p (h d) -> p h d", h=NH)),
                      reads=arc(stv(ci), 2048), writes=[("Vd", q, kb)], dma=True)
                if tile['kind'] == 'prompt':
                    q = qs[0]
                    cumsum_chunk(rows, LF[0:rows, ci, :], [("LF", ci)], C_TRI, [(C_E0, CAR[:, q, :], [("CAR", q)])],
                                 CC[0:rows, ci, :], [("CC", ci)], NCK[0:rows, q, kb, :], [("NCK", q, kb)], (CAR[:, q, :], [("CAR", q)]))
                else:
                    cumsum_chunk(rows, LF[0:rows, ci, :], [("LF", ci)], C_TRIS,
                                 [(C_EA, CAR[:, 2, :], [("CAR", 2)]), (C_EB, CAR[:, 3, :], [("CAR", 3)])],
                                 CC[0:rows, ci, :], [("CC", ci)], NCK[0:rows, 2, kb, :], [("NCK", 2, kb)], None)
            for h in range(NH):
                pb, pc = bank()
                for ci, ch in enumerate(chunks):
                    rows = ch['rows']
                    dg = DG[(h * len(chunks) + ci) % 2]
                    dgc = ("DG", (h * len(chunks) + ci) % 2)
                    A('pool', lambda e, dg=dg, rows=rows, ci=ci, h=h: e.tensor_scalar(out=dg[0:rows, 0:rows], in0=CON[0:rows, C_ID:C_ID + rows], scalar1=CC[0:rows, ci, h:h + 1],
                                                                                     scalar2=None, op0=ALU.mult), reads=["CON", ("CC", ci)], writes=[dgc])
                    A('pe', lambda e, dg=dg, rows=rows, ci=ci, pb=pb: e.matmul(pb[:, ci * 128:ci * 128 + rows], lhsT=CON[0:rows, C_ONES:C_ONES + 128], rhs=dg[0:rows, 0:rows],
                                                                              start=True, stop=True), reads=["CON", dgc], writes=[pc], fp32pe=True)
                cq = CQ_OFF + h * 2048
                A('act', lambda e, pb=pb, cq=cq: e.activation(out=arf(cq, T), in_=pb[:, 0:T], func=AF.Copy), reads=[pc], writes=arc(cq, T * 4))

        def phase_attn(tile):
            T = tile['T']
            scale = HD ** -0.5
            it = [0]
            for si, sg in enumerate(tile['segs']):
                q, c0, n = sg['q'], sg['col0'], sg['n']
                nkb = sg['nkb']
                for h in range(NH):
                    nsl = 4 if nkb * 512 <= 4096 else 2
                    kvo = KV_OFF + (it[0] % nsl) * (16 * KB // nsl)
                    it[0] += 1
                    kth = kvo
                    vh = kvo + (16 * KB // nsl) // 2
                    A('sp', lambda e, q=q, h=h, kth=kth, nkb=nkb: e.dma_start(out=arb(kth, nkb * 128), in_=KTd[q, h][:, 0:nkb * 128]),
                      reads=[("KTd", q, kb) for kb in range(nkb)], writes=arc(kth, nkb * 256), dma=True)
                    A('sp', lambda e, q=q, h=h, vh=vh, nkb=nkb: e.dma_start(out=arb(vh, nkb * 128).rearrange("p (k d) -> p k d", k=nkb), in_=Vd[q, h][:, 0:nkb, :]),
                      reads=[("Vd", q, kb) for kb in range(nkb)], writes=arc(vh, nkb * 256), dma=True)
                    po, poc = bank()
                    pd, pdc = bank()
                    cq = CQ_OFF + h * 2048
                    DEPTH = 3
                    info = {}
                    for step in range(nkb + DEPTH):
                        if step < nkb:
                            kb = step
                            M, mask_ap, nckq = sg['blk'](kb)
                            pss, psc = bank((poc, pdc))
                            A('pe', lambda e, pss=pss, kth=kth, kb=kb, M=M, h=h, c0=c0, n=n: e.matmul(pss[0:M, 0:n], lhsT=arb(kth + kb * 256, M), rhs=arb(QT_OFF + h * 1024 + c0 * 2, n),
                                                                                                       start=True, stop=True),
                              reads=arc(kth + kb * 256, M * 2) + arc(QT_OFF + h * 1024, 1024), writes=[psc])
                            tt = TTV[kb % 4]
                            ttc = TTC[kb % 4]
                            A('dve', lambda e, pss=pss, tt=tt, M=M, cq=cq, c0=c0, n=n: e.scalar_tensor_tensor(out=tt[0:M, 0:n], in0=pss[0:M, 0:n], scalar=scale, in1=arf(cq + c0 * 4, n)[0:M, :],
                                                                                                               op0=ALU.mult, op1=ALU.add),
                              reads=[psc] + arc(cq, T * 4), writes=ttc)
                            if mask_ap is not None:
                                A('pool', lambda e, tt=tt, M=M, n=n, mask_ap=mask_ap: e.tensor_tensor(out=tt[0:M, 0:n], in0=tt[0:M, 0:n], in1=mask_ap, op=ALU.add),
                                  reads=ttc + ["CON"], writes=ttc)
                            pp = PP[kb % 4]
                            ppc = ("PP", kb % 4)
                            A('act', lambda e, tt=tt, pp=pp, M=M, n=n, nckq=nckq, kb=kb, h=h: e.activation(out=pp[0:M, 0:n], in_=tt[0:M, 0:n], func=AF.Exp,
                                                                                                            bias=NCK[0:M, nckq, kb, h:h + 1], scale=1.0),
                              reads=ttc + [("NCK", nckq, kb)], writes=[ppc])
                            info[kb] = (M, pp, ppc)
                        if step >= DEPTH:
                            kb = step - DEPTH
                            M, pp, ppc = info[kb]
                            A('pe', lambda e, po=po, vh=vh, kb=kb, M=M, pp=pp, n=n, nkb=nkb: e.matmul(po[:, 0:n], lhsT=arb(vh + kb * 256, 128)[0:M, :], rhs=pp[0:M, 0:n],
                                                                                                       start=(kb == 0), stop=(kb == nkb - 1)),
                              reads=arc(vh + kb * 256, 256) + [ppc], writes=[poc])
                            A('pe', lambda e, pd=pd, kb=kb, M=M, pp=pp, n=n, nkb=nkb: e.matmul(pd[:, 0:n], lhsT=ONB[0:M, :], rhs=pp[0:M, 0:n],
                                                                                               start=(kb == 0), stop=(kb == nkb - 1)),
                              reads=["ONB", ppc], writes=[pdc])
                    A('dve', lambda e, pd=pd, n=n: e.reciprocal(out=RD[:, 0:n], in_=pd[:, 0:n]), reads=[pdc], writes=["RD"])
                    ao = ATT_OFF + h * 1024 + c0 * 2
                    A('dve', lambda e, po=po, n=n, ao=ao: e.tensor_tensor(out=arb(ao, n), in0=po[:, 0:n], in1=RD[:, 0:n], op=ALU.mult),
                      reads=[poc, "RD"], writes=arc(ATT_OFF + h * 1024, 1024))

        def phase_mix(tile):
            T = tile['T']
            for g2 in range(8):
                bc = wbuf()
                wload_cols("pw", 8, g2 * 256, 256, bc, 0)
                wload_cols("ao", 8, g2 * 256, 256, bc, 2048)
                Wt, wc = bc
                for g in range(2):
                    fo = g2 * 2 + g
                    res = []
                    for (boff, rhs, rc) in [
                        (0, lambda k: arb(ZB_OFF + k * 1024, T), lambda k: arc(ZB_OFF + k * 1024, T * 2)),
                        (2048, lambda k: arb(ATT_OFF + k * 1024, T), lambda k: arc(ATT_OFF + k * 1024, T * 2)),
                    ]:
                        pb, pc = bank()
                        for k in range(8):
                            A('pe', lambda e, pb=pb, Wt=Wt, k=k, g=g, boff=boff, rhs=rhs: e.matmul(pb[:, 0:T], lhsT=Wt[:, boff + k * 256 + g * 128: boff + k * 256 + (g + 1) * 128],
                                                                                                rhs=rhs(k), start=(k == 0), stop=(k == 7)),
                              reads=[wc] + rc(k), writes=[pc])
                        res.append((pb, pc))
                    (pa, pac), (pbo, pboc) = res
                    t1, t2 = T1[fo % 2], T2[fo % 2]
                    ga, gac = gate_ap(0, fo, T)
                    gb, gbc = gate_ap(1, fo, T)
                    A('dve', lambda e, t1=t1, pa=pa, ga=ga: e.tensor_tensor(out=t1[:, 0:T], in0=pa[:, 0:T], in1=ga, op=ALU.mult), reads=[pac] + gac, writes=[("T1", fo % 2)])
                    A('dve', lambda e, t2=t2, pbo=pbo, gb=gb: e.tensor_tensor(out=t2[:, 0:T], in0=pbo[:, 0:T], in1=gb, op=ALU.mult), reads=[pboc] + gbc, writes=[("T2", fo % 2)])
                    mo = MIX_OFF + fo * 1024
                    A('pool', lambda e, t1=t1, t2=t2, mo=mo: e.tensor_tensor(out=arb(mo, T), in0=t1[:, 0:T], in1=t2[:, 0:T], op=ALU.add),
                      reads=[("T1", fo % 2), ("T2", fo % 2)], writes=arc(mo, T * 2))

        def phase_wout(tile):
            for nt in range(4):
                buf = wload_cols("o", 16, nt * 512, 512)
                Wt, wc = buf
                for ci, ch in enumerate(tile['chunks']):
                    rows = ch['rows']
                    pb, pc = bank()
                    for k in range(16):
                        A('pe', lambda e, pb=pb, k=k, ci=ci, rows=rows: e.matmul(pb[0:rows, :], lhsT=arb(MIX_OFF + k * 1024 + ci * 256, rows), rhs=Wt[:, k * 512:(k + 1) * 512],
                                                                                    start=(k == 0), stop=(k == 15)),
                          reads=[wc] + arc(MIX_OFF + k * 1024, 1024), writes=[pc])
                    A('dve', lambda e, pb=pb, ci=ci, rows=rows, nt=nt: e.tensor_tensor(out=X[0:rows, ci, nt * 512:(nt + 1) * 512], in0=pb[0:rows, :], in1=X[0:rows, ci, nt * 512:(nt + 1) * 512],
                                                                                         op=ALU.add), reads=[pc, ("X", ci)], writes=[("X", ci)])

        def phase_ffn(tile):
            T = tile['T']
            segs = tile['segs']
            for f2 in range(NF // 2):
                bgu = wbuf()
                wload_cols("up", 16, f2 * 256, 256, bgu, 0)
                wload_cols("up", 16, DFF + f2 * 256, 256, bgu, 4096)
                Wt, wc = bgu
                for g in range(2):
                    f = f2 * 2 + g
                    res = []
                    for boff in (0, 4096):
                        pb, pc = bank()
                        for k in range(16):
                            A('pe', lambda e, pb=pb, Wt=Wt, k=k, g=g, boff=boff: e.matmul(pb[:, 0:T], lhsT=Wt[:, boff + k * 256 + g * 128:boff + k * 256 + (g + 1) * 128], rhs=HT[:, k, 0:T],
                                                                                  start=(k == 0), stop=(k == 15)),
                              reads=[wc, ("HT", k)], writes=[pc])
                        res.append((pb, pc))
                    (pg, pgc), (pu, puc) = res
                    for si, sg in enumerate(segs):
                        q, c0, n = sg['q'], sg['col0'], sg['n']
                        go = G_OFF + (f % 2) * 4160 + si * 2080
                        ft = FT_OFF + (f % 2) * 2048
                        A('pool', lambda e, go=go, q=q, f=f: e.tensor_copy(out=arf(go, 2), in_=GH[:, q, f, :]), reads=[("GH", q)], writes=arc(go, 8))
                        A('act', lambda e, go=go, pg=pg, c0=c0, n=n: e.activation(out=arf(go + 8, n), in_=pg[:, c0:c0 + n], func=AF.Copy), reads=[pgc], writes=arc(go + 8, n * 4))
                        A('pool', lambda e, go=go, q=q, f=f, n=n: e.tensor_copy(out=GH[:, q, f, :], in_=arf(go + n * 4, 2)), reads=arc(go + n * 4, 8), writes=[("GH", q)])
                        A('dve', lambda e, go=go, ft=ft, f=f, n=n: e.tensor_scalar(out=arf(ft, n), in0=arf(go, n), scalar1=PAR[:, P_FW + f * 3:P_FW + f * 3 + 1],
                                                                                  scalar2=PAR[:, P_FB + f:P_FB + f + 1], op0=ALU.mult, op1=ALU.add),
                          reads=arc(go, n * 4) + ["PAR"], writes=arc(ft, n * 4))
                        for tap in (1, 2):
                            A('dve', lambda e, go=go, ft=ft, f=f, n=n, tap=tap: e.scalar_tensor_tensor(out=arf(ft, n), in0=arf(go + tap * 4, n),
                                                                                                     scalar=PAR[:, P_FW + f * 3 + tap:P_FW + f * 3 + tap + 1], in1=arf(ft, n),
                                                                                                     op0=ALU.mult, op1=ALU.add),
                              reads=arc(go + tap * 4, n * 4) + arc(ft, n * 4) + ["PAR"], writes=arc(ft, n * 4))
                        A('act', lambda e, ft=ft, n=n: e.activation(out=arf(ft, n), in_=arf(ft, n), func=AF.Silu), reads=arc(ft, n * 4), writes=arc(ft, n * 4))
                        ao = ACT_OFF + f * 1024 + c0 * 2
                        A('dve', lambda e, ft=ft, n=n, pu=pu, c0=c0, ao=ao: e.tensor_tensor(out=arb(ao, n), in0=pu[:, c0:c0 + n], in1=arf(ft, n), op=ALU.mult),
                          reads=[puc] + arc(ft, n * 4), writes=arc(ACT_OFF + f * 1024, 1024))
            for si, sg in enumerate(segs):
                if sg['last']:
                    q = sg['q']

                    for f4 in range(NF // 4):
                        fb, fc = bank()
                        for g in range(4):
                            f = f4 * 4 + g
                            A('pe', lambda e, fb=fb, g=g, f=f, q=q: e.transpose(fb[0:2, g * 128:(g + 1) * 128], GH[:, q, f, :], con(C_ID)),
                              reads=[("GH", q), "CON"], writes=[fc])
                        A('act', lambda e, fb=fb: e.activation(out=SQ[0:2, :], in_=fb[0:2, :], func=AF.Copy), reads=[fc], writes=["SQ"])
                        A('sp', lambda e, q=q, f4=f4: e.dma_start(out=FFNOUT[q][:, f4 * 512:(f4 + 1) * 512], in_=SQ[0:2, :]), reads=["SQ"], dma=True)
            chunks = tile['chunks']
            pieces = [(0, 16), (16, 16), (32, 12)]
            for nt in range(4):
                banks = [bank() for _ in chunks]
                for pi, (f0, nf) in enumerate(pieces):
                    buf = wbuf()
                    Wt, wc = buf
                    src = wb["dn"].rearrange("(f p) n -> p f n", p=128)[:, f0:f0 + nf, nt * 512:(nt + 1) * 512]
                    dst = Wt[:, 0:nf * 512].rearrange("p (f n) -> p f n", f=nf)
                    A('sp', lambda e, src=src, dst=dst: e.dma_start(out=dst, in_=src), reads=wcells("dn"), writes=[wc], dma=True)
                    for ci, ch in enumerate(chunks):
                        rows = ch['rows']
                        pb, pc = banks[ci]
                        for j in range(nf):
                            f = f0 + j
                            A('pe', lambda e, pb=pb, Wt=Wt, j=j, f=f, ci=ci, rows=rows: e.matmul(pb[0:rows, :], lhsT=arb(ACT_OFF + f * 1024 + ci * 256, rows), rhs=Wt[:, j * 512:(j + 1) * 512],
                                                                                                  start=(f == 0), stop=(f == NF - 1)),
                              reads=[wc] + arc(ACT_OFF + f * 1024, 1024), writes=[pc])
                for ci, ch in enumerate(chunks):
                    rows = ch['rows']
                    pb, pc = banks[ci]
                    A('dve', lambda e, pb=pb, ci=ci, rows=rows, nt=nt: e.tensor_tensor(out=X[0:rows, ci, nt * 512:(nt + 1) * 512], in0=pb[0:rows, :], in1=X[0:rows, ci, nt * 512:(nt + 1) * 512],
                                                                                         op=ALU.add), reads=[pc, ("X", ci)], writes=[("X", ci)])
            for ci, ch in enumerate(chunks):
                for (p0, n, q, t0) in ch['rg']:
                    A('sp', lambda e, ci=ci, p0=p0, n=n, q=q, t0=t0: e.dma_start(out=YOUT[q][t0:t0 + n, :], in_=X[p0:p0 + n, ci, :]), reads=[("X", ci)], dma=True)

        def sample_past(q):
            b = q - 2
            L0, L1, SK, SV = 44 * KB, 48 * KB, 52 * KB, 54 * KB
            for kb in range(8):
                A('sp', lambda e, kb=kb: e.dma_start(out=arf(L0, 1024), in_=ck[b, kb * 128:(kb + 1) * 128, :]), writes=arc(L0, 4096), dma=True)
                A('dve', lambda e: e.tensor_copy(out=arb(SK, 1024), in_=arf(L0, 1024)), reads=arc(L0, 4096), writes=arc(SK, 2048))
                kt_ingest(128, lambda h: arb(SK + h * 256, 128), arc(SK, 2048), [q], kb)
                A('sp', lambda e, kb=kb: e.dma_start(out=arf(L1, 1024), in_=cv[b, kb * 128:(kb + 1) * 128, :]), writes=arc(L1, 4096), dma=True)
                A('dve', lambda e: e.tensor_copy(out=arb(SV, 1024), in_=arf(L1, 1024)), reads=arc(L1, 4096), writes=arc(SV, 2048))
                A('sp', lambda e, kb=kb: e.dma_start(out=Vd[q].rearrange("h p k d -> p h k d")[:, :, kb, :],
                                                     in_=arb(SV, 1024).rearrange("p (h d) -> p h d", h=NH)),
                  reads=arc(SV, 2048), writes=[("Vd", q, kb)], dma=True)
                ci = kb % 4
                A('sp', lambda e, kb=kb, ci=ci: e.dma_start(out=LF[:, ci, :], in_=clf[b, kb * 128:(kb + 1) * 128, :]), writes=[("LF", ci)], dma=True)
                cumsum_chunk(128, LF[:, ci, :], [("LF", ci)], C_TRI, [(C_E0, CAR[:, q, :], [("CAR", q)])],
                             CC[:, ci, :], [("CC", ci)], NCK[:, q, kb, :], [("NCK", q, kb)], (CAR[:, q, :], [("CAR", q)]))

        def mask_prompt(i, n):
            o = C_MASK + 384 - 128 * i
            return CON[:, o:o + n]

        tiles = []
        for s in range(2):
            for j in range(4):
                def blk(kb, j=j, s=s):
                    if kb < 4 * j:
                        return 128, None, s
                    return 128, mask_prompt(kb - 4 * j, 512), s
                tiles.append(dict(kind='prompt', T=512,
                                  chunks=[dict(rows=128, rg=[(0, 128, s, j * 512 + c * 128)], kb=j * 4 + c, qs=[s]) for c in range(4)],
                                  segs=[dict(kind='prompt', q=s, col0=0, n=512, first=(j == 0), last=(j == 3), nkb=4 * j + 4, blk=blk)]))

        def blk_s(kb, q):
            if kb < 8:
                return 128, None, q
            c0 = 0 if q == 2 else 32
            return 48, CON[0:48, C_MASKS + c0:C_MASKS + c0 + 16], 2
        tiles.append(dict(kind='sample', T=64,
                          chunks=[dict(rows=64, rg=[(0, 16, 2, 0), (32, 16, 3, 0)], kb=8, qs=[2, 3])],
                          segs=[dict(kind='sample', q=2, col0=0, n=16, first=False, last=True, nkb=9, blk=lambda kb: blk_s(kb, 2)),
                                dict(kind='sample', q=3, col0=32, n=16, first=False, last=True, nkb=9, blk=lambda kb: blk_s(kb, 3))]))

        if tile_sel is not None:
            tiles = [tiles[i] for i in tile_sel]
        early_past = any(t['kind'] == 'sample' for t in tiles) and os.environ.get('DBG_EARLY_PAST', '1') == '1'
        if early_past:
            for q in (2, 3):
                A('sp', lambda e, q=q: e.dma_start(out=arf(44 * KB, 1024)[0:30, :], in_=stc[q - 2]), writes=arc(44 * KB, 4096), dma=True)
                pb, pc = bank()
                for c in range(8):
                    A('pe', lambda e, pb=pb, c=c: e.transpose(pb[:, c * 32:c * 32 + 30], arf(44 * KB + c * 512, 128)[0:30, :], CON[0:30, C_ID:C_ID + 30]),
                      reads=arc(44 * KB, 4096) + ["CON"], writes=[pc])
                A('act', lambda e, pb=pb, q=q: e.activation(out=UHS[:, q - 2, :, :], in_=pb[:, 0:256].rearrange("p (c t) -> p c t", c=8)[:, :, 0:30], func=AF.Copy),
                  reads=[pc], writes=[("UHS", q - 2, c) for c in range(8)])
                A('sp', lambda e, q=q: e.dma_start(out=arf(0, DFF)[0:2, :], in_=stf[q - 2]), writes=arc(0, DFF * 4), dma=True)
                pb, pc = bank()
                for f in range(NF):
                    A('pe', lambda e, pb=pb, f=f: e.transpose(pb[:, f * 2:f * 2 + 2], arf(f * 512, 128)[0:2, :], CON[0:2, C_ID:C_ID + 2]),
                      reads=arc(f * 512, 512) + ["CON"], writes=[pc])
                A('act', lambda e, pb=pb, q=q: e.activation(out=GH[:, q, :, :].rearrange("p f t -> p (f t)"), in_=pb[:, 0:2 * NF], func=AF.Copy),
                  reads=[pc], writes=[("GH", q)])
            for q in (2, 3):
                sample_past(q)
        for nm in ["in", "pw", "ao", "o", "up", "dn"]:
            wconv(nm)
        for tile in tiles:
            if os.environ.get('DBG_BARRIER', '1') == '1':
                S.barrier()
            if tile['kind'] == 'sample':
                A('pool', lambda e: e.memset(X[:, 0, :], 0.0), writes=[("X", 0)])
                assert early_past
                A('pool', lambda e: e.memset(ARf[:], 0.0), writes=arc(0, ARENA_KB * KB))
            load_x(tile)
            if upto >= 1:
                norm_transpose(tile, P_G1)
                if os.environ.get('DBG_PB', '1') == '1':
                    S.barrier()
            if upto >= 2:
                phase_glu_conv(tile)
                phase_gate_proj(tile)
                phase_conv_ln(tile)
                if os.environ.get('DBG_PB', '1') == '1':
                    S.barrier()
            if upto >= 3:
                phase_qkv(tile)
                if os.environ.get('DBG_PB', '1') == '1':
                    S.barrier()
            if upto >= 4:
                if tile['kind'] == 'sample':
                    A('pool', lambda e: e.memset(arb(ATT_OFF, 4096), 0.0), writes=arc(ATT_OFF, 8192))
                phase_attn(tile)
                if os.environ.get('DBG_PB', '1') == '1':
                    S.barrier()
            if upto >= 5:
                phase_mix(tile)
                if os.environ.get('DBG_PB', '1') == '1':
                    S.barrier()
            if upto >= 6:
                if tile['kind'] == 'sample':
                    A('pool', lambda e: e.memset(X[:, 0, :], 0.0), writes=[("X", 0)])
                load_x(tile)
                phase_wout(tile)
                if os.environ.get('DBG_PB', '1') == '1':
                    S.barrier()
            if upto == 6:
                for ci, ch in enumerate(tile['chunks']):
                    for (p0, n, q, t0) in ch['rg']:
                        A('sp', lambda e, ci=ci, p0=p0, n=n, q=q, t0=t0: e.dma_start(out=YOUT[q][t0:t0 + n, :], in_=X[p0:p0 + n, ci, :]), reads=[("X", ci)], dma=True)
            if upto >= 7:
                norm_transpose(tile, P_G2)
                if os.environ.get('DBG_PB', '1') == '1':
                    S.barrier()
            if upto >= 8:
                if tile['kind'] == 'sample':
                    A('pool', lambda e: e.memset(arb(ACT_OFF, 44 * 512), 0.0), writes=arc(ACT_OFF, 44 * 1024))
                phase_ffn(tile)

        if dbg2 is not None:
            A('sp', lambda e: e.dma_start(out=dbg2[:, 0:4 * KBMAX * 8], in_=NCK[:].rearrange("p a b c -> p (a b c)")), reads=[("NCK", q, kb) for q in range(4) for kb in range(KBMAX)], dma=True)
            A('sp', lambda e: e.dma_start(out=dbg2[:, 4 * KBMAX * 8:], in_=CAR[:].rearrange("p a c -> p (a c)")), reads=[("CAR", q) for q in range(4)], dma=True)
        S.emit(nc, block, csem, dsem)
    return nc


_NC = [None]


def kernel(x_prompt, x_sample, cache_k, cache_v, cache_logf, state_conv, state_ffn,
           norm_mix_g, w_in, b_forget, conv_dw_w, conv_dw_b, conv_ln_g, conv_ln_b,
           conv_pw_out, q_norm_g, k_norm_g, w_att_out, w_out,
           norm_ffn_g, w_up, ffn_dw_w, ffn_dw_b, w_down):
    f = lambda a: np.ascontiguousarray(np.asarray(a, dtype=np.float32))
    x_prompt, x_sample = f(x_prompt), f(x_sample)
    ckk = f(cache_k)[0].reshape(16, PAST, AW)
    cvv = f(cache_v)[0].reshape(16, PAST, AW)
    clff = f(cache_logf)[0]
    stc = f(state_conv)[0]
    stf = f(state_ffn)[0]
    params = make_params(f(norm_mix_g)[0], f(b_forget)[0], f(conv_dw_w)[0], f(conv_dw_b)[0], f(conv_ln_g)[0], f(conv_ln_b)[0],
                         f(q_norm_g)[0], f(k_norm_g)[0], f(norm_ffn_g)[0], f(ffn_dw_w)[0], f(ffn_dw_b)[0])
    consts = make_consts()
    W = {"w_in": f(w_in)[0], "w_pw": f(conv_pw_out)[0], "w_ao": f(w_att_out)[0], "w_o": f(w_out)[0], "w_up": f(w_up)[0], "w_dn": f(w_down)[0]}
    if _NC[0] is None:
        _NC[0] = build_program()
    nc = _NC[0]
    in_maps = []
    for c in range(8):
        m = {"xp": x_prompt[2 * c:2 * c + 2], "xs": x_sample[2 * c:2 * c + 2], "ck": ckk[2 * c:2 * c + 2], "cv": cvv[2 * c:2 * c + 2],
             "clf": clff[2 * c:2 * c + 2], "stc": stc[2 * c:2 * c + 2], "stf": stf[2 * c:2 * c + 2], "params": params, "consts": consts}
        m.update(W)
        in_maps.append(m)
    res = run_bass_kernel_spmd(nc, in_maps, core_ids=list(range(8)))
    R = res.results
    cat = lambda k: np.concatenate([np.asarray(r[k]) for r in R], axis=0)
    y_p = cat("yp")
    y_s = cat("ys")
    p_k = cat("pk").reshape(1, 16, SEQ, NH, HD)
    p_v = cat("pv").reshape(1, 16, SEQ, NH, HD)
    p_lf = cat("plf").reshape(1, 16, SEQ, NH)
    p_c = cat("pconv").reshape(1, 16, CW - 1, DC)
    p_f = cat("pffn").reshape(1, 16, 2, DFF)
    s_k = cat("sk").reshape(1, 16, DSEQ, NH, HD)
    s_v = cat("sv").reshape(1, 16, DSEQ, NH, HD)
    s_lf = cat("slf").reshape(1, 16, DSEQ, NH)
    s_c = cat("sconv").reshape(1, 16, CW - 1, DC)
    s_f = cat("sffn").reshape(1, 16, 2, DFF)
    return (y_p, y_s, p_k, p_v, p_lf, p_c, p_f, s_k, s_v, s_lf, s_c, s_f)
```

```python
import numpy as np
import os
import types
from contextlib import ExitStack
import concourse.bass as bass
import concourse.mybir as mybir
from concourse.bass_utils import run_bass_kernel_spmd

F32 = mybir.dt.float32
BF16 = mybir.dt.bfloat16
AF = mybir.ActivationFunctionType
ALU = mybir.AluOpType
AX = mybir.AxisListType
ENGS = ['pe', 'act', 'dve', 'pool', 'sp']

D = 2048
SEQ = 2048
NH = 8
HD = 128
AW = 1024
DC = 1024
CW = 31
DFF = 5632
NF = DFF // 128
NIN = 2 * DC + 3 * AW + NH + 2 * D
O_Q = 2 * DC
O_K = O_Q + AW
O_V = O_K + AW
O_F = O_V + AW
O_GA = O_F + NH
O_GB = O_GA + D
EPS = 1e-6
NEG = -30000.0
PAST = 1024
DSEQ = 16
KBMAX = 17


def _freeze(fn):
    if fn.__closure__ is None:
        return fn
    cells = []
    for c in fn.__closure__:
        try:
            cells.append(types.CellType(c.cell_contents))
        except ValueError:
            cells.append(c)
    g = types.FunctionType(fn.__code__, fn.__globals__, fn.__name__, fn.__defaults__, tuple(cells))
    g.__kwdefaults__ = fn.__kwdefaults__
    return g


class Sched:
    R = int(os.environ.get("DBG_R", "8"))

    def __init__(self):
        self.ops = []
        self.cells = {}
        self.eng_ops = {e: [] for e in ENGS}
        self.known = {e: {f: -1 for f in ENGS} for e in ENGS}
        self.known_dma = {e: set() for e in ENGS}
        self.dma_q = {e: [] for e in ENGS}

    def add(self, eng, fn, reads=(), writes=(), dma=False, extra_deps=()):
        idx = len(self.ops)
        op = dict(idx=idx, eng=eng, fn=_freeze(fn), dma=dma, seq=len(self.eng_ops[eng]),
                  deps_c={}, deps_d=[], flagged=False, val=None, slot=None)
        deps = set()
        for c in reads:
            st = self.cells.get(c)
            if st is None:
                st = self.cells[c] = [None, {}, []]
            if st[0] is not None:
                deps.add(st[0])
        for c in writes:
            st = self.cells.get(c)
            if st is None:
                st = self.cells[c] = [None, {}, []]
            if st[0] is not None:
                deps.add(st[0])
            deps.update(st[1].values())
            deps.update(st[2])
            st[0] = idx
            st[1] = {}
            st[2] = []
        wset = set(writes)
        for c in reads:
            if c in wset:
                continue
            st = self.cells[c]
            if dma:
                st[2].append(idx)
            else:
                st[1][eng] = idx
        if dma:
            q = self.dma_q[eng]
            if len(q) >= self.R:
                deps.add(q[len(q) - self.R])
            q.append(idx)
        deps.update(extra_deps)
        deps.discard(idx)
        kn = self.known[eng]
        for d in sorted(deps):
            od = self.ops[d]
            if od['dma']:
                if d in self.known_dma[eng]:
                    continue
                self.known_dma[eng].add(d)
                op['deps_d'].append(d)
            else:
                f = od['eng']
                if f == 'pe' and eng == 'pe':
                    continue
                if kn[f] >= od['seq']:
                    continue
                if op['deps_c'].get(f, -1) < od['seq']:
                    op['deps_c'][f] = od['seq']
        for f, s in op['deps_c'].items():
            kn[f] = s
            self.ops[self.eng_ops[f][s]]['flagged'] = True
        self.ops.append(op)
        self.eng_ops[eng].append(idx)
        return idx

    def barrier(self):
        last = [self.eng_ops[e][-1] for e in ENGS if self.eng_ops[e]]
        last = [i for i in last if not self.ops[i]['dma']]
        for e in ENGS:
            comp = [i for i in reversed(self.eng_ops[e]) if not self.ops[i]['dma']]
            if comp and comp[0] not in last:
                last.append(comp[0])
        dmas = [i for e in ENGS for i in self.dma_q[e][-self.R:]]
        for e in os.environ.get('DBG_BAR_ENGS', 'act').split(','):
            self.add(e, lambda eng: eng.nop(), extra_deps=last + dmas)

    def emit(self, nc, block, csem, dsem):
        R = self.R
        for e in ENGS:
            cnt = 0
            for idx in self.eng_ops[e]:
                op = self.ops[idx]
                if op['dma']:
                    continue
                if op['flagged']:
                    cnt += 1
                    op['val'] = cnt
            for i, idx in enumerate(self.dma_q[e]):
                op = self.ops[idx]
                op['slot'] = i % R
                op['val'] = 16 * (i // R + 1)

        def run(e, eng):
            for idx in self.eng_ops[e]:
                op = self.ops[idx]
                for f, s in op['deps_c'].items():
                    od = self.ops[self.eng_ops[f][s]]
                    eng.wait_ge(csem[f], od['val'])
                for d in op['deps_d']:
                    od = self.ops[d]
                    eng.wait_ge(dsem[od['eng']][od['slot']], od['val'])
                ins = op['fn'](eng)
                if op['dma']:
                    ins.then_inc(dsem[e][op['slot']], 16)
                elif op['flagged']:
                    ins.then_inc(csem[e], 1)
            if e == 'sp':
                for q in ENGS:
                    for idx in self.dma_q[q][-R:]:
                        od = self.ops[idx]
                        eng.wait_ge(dsem[q][od['slot']], od['val'])
                for f in ENGS:
                    if f == 'sp':
                        continue
                    last = None
                    for idx in self.eng_ops[f]:
                        if self.ops[idx]['val'] is not None and not self.ops[idx]['dma']:
                            last = self.ops[idx]
                    if last is not None:
                        eng.wait_ge(csem[f], last['val'])

        @block.tensor
        def _(eng):
            run('pe', eng)

        @block.scalar
        def _(eng):
            run('act', eng)

        @block.vector
        def _(eng):
            run('dve', eng)

        @block.gpsimd
        def _(eng):
            run('pool', eng)

        @block.sync
        def _(eng):
            run('sp', eng)


C_ID, C_TRI, C_TRIS, C_E0, C_EA, C_EB, C_ONES, C_ONESM, C_MASK, C_MASKS = 0, 128, 256, 384, 512, 640, 768, 896, 1024, 1920
NCON = 1984
P_CW, P_CB, P_LG, P_LB, P_FW, P_FB, P_G1, P_G2, P_QG, P_KG, P_BF = 0, 248, 256, 264, 272, 404, 448, 464, 480, 608, 736
NPAR = 744


def make_consts():
    c = np.zeros((128, NCON), np.float32)
    p = np.arange(128)
    c[:, C_ID:C_ID + 128] = np.eye(128)
    c[:, C_TRI:C_TRI + 128] = (p[:, None] <= p[None, :])
    seg = np.full(128, -1)
    seg[0:16] = 0
    seg[32:48] = 1
    same = (seg[:, None] == seg[None, :]) & (seg[:, None] >= 0)
    c[:, C_TRIS:C_TRIS + 128] = same & (p[:, None] <= p[None, :])
    c[0, C_E0:C_E0 + 128] = 1.0
    c[0, C_EA:C_EA + 16] = 1.0
    c[0, C_EB + 32:C_EB + 48] = 1.0
    c[:, C_ONES:C_ONES + 128] = 1.0
    c[:, C_ONESM:C_ONESM + 128] = 1.0 / DC
    j = np.arange(896)
    c[:, C_MASK:C_MASK + 896] = np.where(p[:, None] <= j[None, :] - 384, 0.0, NEG)
    c[:, C_MASKS:C_MASKS + 64] = np.where(same[:, 0:64] & (p[:, None] <= p[None, 0:64]), 0.0, NEG)
    return c


def make_params(norm_mix_g, b_forget, conv_dw_w, conv_dw_b, conv_ln_g, conv_ln_b, q_norm_g, k_norm_g,
                norm_ffn_g, ffn_dw_w, ffn_dw_b):
    P = np.zeros((128, NPAR), np.float32)
    P[:, P_CW:P_CW + 248] = conv_dw_w.reshape(CW, 8, 128).transpose(2, 1, 0).reshape(128, 248)
    P[:, P_CB:P_CB + 8] = conv_dw_b.reshape(8, 128).T
    P[:, P_LG:P_LG + 8] = conv_ln_g.reshape(8, 128).T
    P[:, P_LB:P_LB + 8] = conv_ln_b.reshape(8, 128).T
    P[:, P_FW:P_FW + 132] = ffn_dw_w.reshape(3, NF, 128).transpose(2, 1, 0).reshape(128, 132)
    P[:, P_FB:P_FB + NF] = ffn_dw_b.reshape(NF, 128).T
    P[:, P_G1:P_G1 + 16] = norm_mix_g.reshape(16, 128).T
    P[:, P_G2:P_G2 + 16] = norm_ffn_g.reshape(16, 128).T
    P[:, P_QG:P_QG + 128] = q_norm_g.reshape(1, 128)
    P[:, P_KG:P_KG + 128] = k_norm_g.reshape(1, 128)
    P[:, P_BF:P_BF + 8] = b_forget.reshape(1, 8)
    return P


def build_program(tile_sel=None, upto=99, conv_w=True):
    nc = bass.Bass("TRN2", target_bir_lowering=False)
    S = Sched()

    def din(name, shape, dt=F32):
        return nc.dram_tensor(name, list(shape), dt, kind="ExternalInput").ap()

    def dout(name, shape):
        return nc.dram_tensor(name, list(shape), F32, kind="ExternalOutput").ap()

    def dscr(name, shape, dt):
        return nc.dram_tensor(name, list(shape), dt, kind="Internal").ap()

    xp = din("xp", [2, SEQ, D])
    xs = din("xs", [2, DSEQ, D])
    ck = din("ck", [2, PAST, AW])
    cv = din("cv", [2, PAST, AW])
    clf = din("clf", [2, PAST, NH])
    stc = din("stc", [2, CW - 1, DC])
    stf = din("stf", [2, 2, DFF])
    wf = {"in": din("w_in", [D, NIN]), "pw": din("w_pw", [DC, D]), "ao": din("w_ao", [AW, D]),
          "o": din("w_o", [D, D]), "up": din("w_up", [D, 2 * DFF]), "dn": din("w_dn", [DFF, D])}
    params = din("params", [128, NPAR])
    consts = din("consts", [128, NCON])
    yp = dout("yp", [2, SEQ, D])
    ys = dout("ys", [2, DSEQ, D])
    pk = dout("pk", [2, SEQ, AW])
    pv = dout("pv", [2, SEQ, AW])
    plf = dout("plf", [2, SEQ, NH])
    pconv = dout("pconv", [2, CW - 1, DC])
    pffn = dout("pffn", [2, 2, DFF])
    sk = dout("sk", [2, DSEQ, AW])
    sv = dout("sv", [2, DSEQ, AW])
    slf = dout("slf", [2, DSEQ, NH])
    sconv = dout("sconv", [2, CW - 1, DC])
    sffn = dout("sffn", [2, 2, DFF])
    wb = {k: dscr("wb_" + k, v.shape, BF16) for k, v in wf.items()}
    dbg = nc.dram_tensor("dbg", [6, 128, 8192], BF16, kind="ExternalOutput").ap() if os.environ.get("DBG_WB") else None
    dbg2 = nc.dram_tensor("dbg2", [128, 4 * KBMAX * 8 + 32], F32, kind="ExternalOutput").ap() if os.environ.get("DBG2") else None
    KTd = dscr("KTd", [4, NH, 128, KBMAX * 128], BF16)
    Vd = dscr("Vd", [4, NH, 128, KBMAX, 128], BF16)

    XIN = [xp[0], xp[1], xs[0], xs[1]]
    YOUT = [yp[0], yp[1], ys[0], ys[1]]
    KOUT = [pk[0], pk[1], sk[0], sk[1]]
    VOUT = [pv[0], pv[1], sv[0], sv[1]]
    LFOUT = [plf[0], plf[1], slf[0], slf[1]]
    CONVOUT = [pconv[0], pconv[1], sconv[0], sconv[1]]
    FFNOUT = [pffn[0], pffn[1], sffn[0], sffn[1]]

    es = ExitStack()
    with es:
        def sb(name, shape, dt):
            return es.enter_context(nc.sbuf_tensor(name, list(shape), dt))

        def ps(name, shape, dt):
            return es.enter_context(nc.psum_tensor(name, list(shape), dt))

        CON = sb("CON", [128, NCON], F32)
        PAR = sb("PAR", [128, NPAR], F32)
        IDB = sb("IDB", [128, 128], BF16)
        ONB = sb("ONB", [128, 128], BF16)
        X = sb("X", [128, 4, D], F32)
        HT = sb("HT", [128, 16, 512], BF16)
        HB = sb("HB", [128, D], BF16)
        NWB = 3
        WB = [sb("WB%d" % i, [128, 8192], BF16) for i in range(NWB)]
        SS = sb("SS", [128, 8], F32)
        RS = sb("RS", [128, 8], F32)
        JUNK = sb("JUNK", [128, D], BF16)
        UH = sb("UH", [128, 8, 30], F32)
        UHS = sb("UHS", [128, 2, 8, 30], F32)
        GH = sb("GH", [128, 4, NF, 2], F32)
        NCK = sb("NCK", [128, 4, KBMAX, 8], F32)
        CAR = sb("CAR", [128, 4, 8], F32)
        LF = sb("LF", [128, 4, 8], F32)
        CC = sb("CC", [128, 4, 8], F32)
        SM = sb("SM", [128, 64], F32)
        DG = [sb("DG%d" % i, [128, 128], F32) for i in range(2)]
        TT = [sb("TT%d" % i, [128, 512], F32) for i in range(2)]
        PP = [sb("PP%d" % i, [128, 512], BF16) for i in range(4)]
        RD = sb("RD", [128, 512], F32)
        JF = JUNK[:].bitcast(F32)
        TTV = [TT[0][:, :], TT[1][:, :], JF[:, 0:512], JF[:, 512:1024]]
        TTC = [[("TT", 0)], [("TT", 1)], ["JUNK"], ["JUNK2"]]
        T1 = [sb("T1_%d" % i, [128, 512], F32) for i in range(2)]
        T2 = [sb("T2_%d" % i, [128, 512], F32) for i in range(2)]
        KF = [sb("KF%d" % i, [128, 512], F32) for i in range(2)]
        SQ = sb("SQ", [128, 512], F32)
        ARENA_KB = 60
        ARf = sb("ARf", [128, ARENA_KB * 256], F32)
        ARb = ARf[:].bitcast(BF16)
        PSB = [ps("PS%d" % i, [128, 512], F32) for i in range(6)]
        PTB = [ps("PT%d" % i, [128, 1024], BF16) for i in range(2)]
        csem = {e: es.enter_context(nc.semaphore("c_" + e)) for e in ENGS}
        dsem = {e: [es.enter_context(nc.semaphore("d_%s%d" % (e, i))) for i in range(Sched.R)] for e in ENGS}
        block = es.enter_context(nc.Block())

        _A = S.add
        NOFP32 = os.environ.get("DBG_NOFP32")

        def A(eng, fn, reads=(), writes=(), dma=False, fp32pe=False):
            if fp32pe and NOFP32:
                return None
            return _A(eng, fn, reads, writes, dma)
        con = lambda off, n=128, rows=128: CON[0:rows, off:off + n]

        def arc(off_b, nbytes):
            return [("AR", b) for b in range(off_b // 256, (off_b + nbytes + 255) // 256)]

        def arf(off_b, n):
            return ARf[:, off_b // 4: off_b // 4 + n]

        def arb(off_b, n):
            return ARb[:, off_b // 2: off_b // 2 + n]

        KB = 1024
        U_OFF, U_STRIDE = 0, 576 * 4
        Z_OFF = 18 * KB
        ZB_OFF = 34 * KB
        LN_OFF = 42 * KB
        ST_OFF = 0
        QT_OFF = 24 * KB
        CQ_OFF = 42 * KB
        ATT_OFF = 0
        KV_OFF = 8 * KB
        MIX_OFF = 8 * KB
        ACT_OFF = 0
        G_OFF = 44 * KB
        FT_OFF = 53 * KB
        KST_OFF = 58 * KB

        XBv = X[:].bitcast(BF16)

        def gate_ap(which, fo, T):
            c = which * 2 + fo // 8
            return XBv[:, c, (fo % 8) * 512:(fo % 8) * 512 + T], [("X", c)]

        psn = [0]

        def bank(excl=()):
            while True:
                b = psn[0] % 6
                psn[0] += 1
                if ("PS", b) not in excl:
                    return PSB[b], ("PS", b)

        ptn = [0]

        def ptbank():
            b = ptn[0] % 2
            ptn[0] += 1
            return PTB[b], ("PT", b)

        wbn = [0]

        def wbuf():
            b = wbn[0] % NWB
            wbn[0] += 1
            return WB[b], ("WB", b)

        A('sp', lambda e: e.dma_start(out=CON[:], in_=consts), writes=["CON"], dma=True)
        A('sp', lambda e: e.dma_start(out=PAR[:], in_=params), writes=["PAR"], dma=True)
        A('dve', lambda e: e.tensor_copy(out=IDB[:], in_=con(C_ID)), reads=["CON"], writes=["IDB"])
        A('dve', lambda e: e.tensor_copy(out=ONB[:], in_=con(C_ONES)), reads=["CON"], writes=["ONB"])
        A('pool', lambda e: e.memset(X[:], 0.0), writes=[("X", c) for c in range(4)])
        A('pool', lambda e: e.memset(GH[:], 0.0), writes=[("GH", q) for q in range(4)])
        A('pool', lambda e: e.memset(CAR[:], 0.0), writes=[("CAR", q) for q in range(4)])
        A('pool', lambda e: e.memset(ARf[:], 0.0), writes=arc(0, ARENA_KB * KB))

        def wconv(name):
            Kr, Nc = wf[name].shape
            for r0 in range(0, Kr, 256):
                r1 = min(Kr, r0 + 256)
                A('pool', lambda e, r0=r0, r1=r1: e.dma_start(out=wb[name][r0:r1, :], in_=wf[name][r0:r1, :]),
                  writes=[("wb", name, r0 // 256)], dma=True)

        def wcells(name, c0=0, c1=0):
            Kr = wf[name].shape[0]
            return [("wb", name, r) for r in range((Kr + 255) // 256)]

        def wload_cols(name, nk, c0, ncols, buf=None, boff=0):
            if buf is None:
                buf = wbuf()
            Wt, wc = buf
            src = wb[name].rearrange("(k p) n -> p k n", p=128)[:, 0:nk, c0:c0 + ncols]
            dst = Wt[:, boff:boff + nk * ncols].rearrange("p (k n) -> p k n", k=nk)
            A('sp', lambda e: e.dma_start(out=dst, in_=src), reads=wcells(name, c0, c0 + ncols), writes=[wc], dma=True)
            return buf

        def norm_transpose(tile, gcol):
            for ci, ch in enumerate(tile['chunks']):
                rows = ch['rows']
                A('pool', lambda e, ci=ci: e.memset(SS[:, ci:ci + 1], 0.0), writes=[("SS", ci)])
                A('act', lambda e, ci=ci, rows=rows: e.activation(out=JUNK[0:rows, :], in_=X[0:rows, ci, :], func=AF.Square,
                                                                  accum_out=SS[0:rows, ci:ci + 1]),
                  reads=[("X", ci), ("SS", ci)], writes=["JUNK", "JUNK2", ("SS", ci)])
                A('act', lambda e, ci=ci: e.activation(out=RS[:, ci:ci + 1], in_=SS[:, ci:ci + 1], func=AF.Ln, bias=EPS, scale=1.0 / D),
                  reads=[("SS", ci)], writes=[("RS", ci)])
                A('act', lambda e, ci=ci: e.activation(out=RS[:, ci:ci + 1], in_=RS[:, ci:ci + 1], func=AF.Exp, scale=-0.5),
                  reads=[("RS", ci)], writes=[("RS", ci)])
                A('dve', lambda e, ci=ci, rows=rows: e.tensor_scalar(out=HB[0:rows, :], in0=X[0:rows, ci, :], scalar1=RS[0:rows, ci:ci + 1],
                                                                     scalar2=None, op0=ALU.mult),
                  reads=[("X", ci), ("RS", ci)], writes=["HB"])
                for g8 in range(2):
                    pt, ptc = ptbank()
                    for k in range(8):
                        fi = g8 * 8 + k
                        A('pe', lambda e, pt=pt, k=k, fi=fi, rows=rows: e.transpose(pt[:, k * 128:k * 128 + rows], HB[0:rows, fi * 128:(fi + 1) * 128],
                                                                                    IDB[0:rows, 0:rows]),
                          reads=["HB", "IDB"], writes=[ptc])
                    for k in range(8):
                        fi = g8 * 8 + k
                        eng = 'dve'
                        if eng == 'act':
                            fn = lambda e, pt=pt, k=k, fi=fi, rows=rows, ci=ci: e.activation(
                                out=HT[:, fi, ci * 128:ci * 128 + rows], in_=pt[:, k * 128:k * 128 + rows], func=AF.Copy,
                                scale=PAR[:, gcol + fi:gcol + fi + 1])
                        else:
                            fn = lambda e, pt=pt, k=k, fi=fi, rows=rows, ci=ci: e.tensor_scalar(
                                out=HT[:, fi, ci * 128:ci * 128 + rows], in0=pt[:, k * 128:k * 128 + rows],
                                scalar1=PAR[:, gcol + fi:gcol + fi + 1], scalar2=None, op0=ALU.mult)
                        A(eng, fn, reads=[ptc, "PAR"], writes=[("HT", fi)])

        def load_x(tile):
            for ci, ch in enumerate(tile['chunks']):
                for (p0, n, q, t0) in ch['rg']:
                    A('sp', lambda e, ci=ci, p0=p0, n=n, q=q, t0=t0: e.dma_start(out=X[p0:p0 + n, ci, :], in_=XIN[q][t0:t0 + n, :]),
                      writes=[("X", ci)], dma=True)

        def ws_group(name, nk, c0, ngrp, rhs_fn, rhs_cells, T, buf=None, boff=0):
            buf = wload_cols(name, nk, c0, ngrp * 128, buf, boff)
            Wt, wc = buf
            outs = []
            for g in range(ngrp):
                pb, pc = bank()
                for k in range(nk):
                    A('pe', lambda e, pb=pb, k=k, g=g: e.matmul(pb[:, 0:T], lhsT=Wt[:, boff + k * ngrp * 128 + g * 128: boff + k * ngrp * 128 + (g + 1) * 128],
                                                                  rhs=rhs_fn(k), start=(k == 0), stop=(k == nk - 1)),
                      reads=[wc] + rhs_cells(k), writes=[pc])
                outs.append((pb, pc))
            return outs

        def useg(si, c):
            off = U_OFF + c * U_STRIDE if si == 0 else U_OFF + c * U_STRIDE + 64 * 4
            return off

        def phase_glu_conv(tile):
            T = tile['T']
            segs = tile['segs']
            for si, sg in enumerate(segs):
                q = sg['q']
                for c in range(8):
                    off = useg(si, c)
                    if sg['kind'] == 'prompt':
                        if sg['first']:
                            A('pool', lambda e, off=off: e.memset(arf(off, 30), 0.0), writes=arc(off, 120))
                        else:
                            A('pool', lambda e, off=off, c=c: e.tensor_copy(out=arf(off, 30), in_=UH[:, c, :]), reads=[("UH", c)], writes=arc(off, 120))
                    else:
                        A('pool', lambda e, off=off, c=c, q=q: e.tensor_copy(out=arf(off, 30), in_=UHS[:, q - 2, c, :]), reads=[("UHS", q - 2, c)], writes=arc(off, 120))
            hrhs = lambda k: HT[:, k, 0:T]
            hcells = lambda k: [("HT", k)]
            for c2 in range(4):
                buf = wbuf()
                wload_cols("in", 16, c2 * 256, 256, buf, 0)
                wload_cols("in", 16, DC + c2 * 256, 256, buf, 4096)
                Wt, wc = buf
                pbs = []
                for half in range(2):
                    for g in range(2):
                        pb, pc = bank()
                        boff = half * 4096
                        for k in range(16):
                            A('pe', lambda e, pb=pb, k=k, g=g, boff=boff: e.matmul(pb[:, 0:T], lhsT=Wt[:, boff + k * 256 + g * 128: boff + k * 256 + (g + 1) * 128],
                                                                                     rhs=hrhs(k), start=(k == 0), stop=(k == 15)),
                              reads=[wc] + hcells(k), writes=[pc])
                        pbs.append((pb, pc))
                for g in range(2):
                    c = c2 * 2 + g
                    (pa, pac), (pbb, pbc) = pbs[g], pbs[2 + g]
                    sgt = T1[g]
                    A('act', lambda e, sgt=sgt, pbb=pbb: e.activation(out=sgt[:, 0:T], in_=pbb[:, 0:T], func=AF.Sigmoid), reads=[pbc], writes=[("T1", g)])
                    for si, sg in enumerate(segs):
                        off = useg(si, c) + 120
                        c0, n = sg['col0'], sg['n']
                        A('dve', lambda e, off=off, pa=pa, sgt=sgt, c0=c0, n=n: e.tensor_tensor(out=arf(off, n), in0=pa[:, c0:c0 + n], in1=sgt[:, c0:c0 + n], op=ALU.mult),
                          reads=[pac, ("T1", g)], writes=arc(off, n * 4))
        def phase_gate_proj(tile):
            T = tile['T']
            for g2 in range(8):
                ba = wbuf()
                wload_cols("in", 16, O_GA + g2 * 256, 256, ba, 0)
                wload_cols("in", 16, O_GB + g2 * 256, 256, ba, 4096)
                Wt, wc = ba
                for g in range(2):
                    fo = g2 * 2 + g
                    for which, boff in ((0, 0), (1, 4096)):
                        pb, pc = bank()
                        for k in range(16):
                            A('pe', lambda e, pb=pb, Wt=Wt, k=k, g=g, boff=boff: e.matmul(pb[:, 0:T], lhsT=Wt[:, boff + k * 256 + g * 128: boff + k * 256 + (g + 1) * 128],
                                                                                          rhs=HT[:, k, 0:T], start=(k == 0), stop=(k == 15)),
                              reads=[wc, ("HT", k)], writes=[pc])
                        gap, gcells = gate_ap(which, fo, T)
                        A('act', lambda e, gap=gap, pb=pb: e.activation(out=gap, in_=pb[:, 0:T], func=AF.Sigmoid), reads=[pc], writes=gcells)

        def phase_conv_ln(tile):
            T = tile['T']
            segs = tile['segs']
            for si, sg in enumerate(segs):
                q, n, c0 = sg['q'], sg['n'], sg['col0']
                for c in range(8):
                    off = useg(si, c)
                    if sg['last']:
                        if c % 4 == 0:
                            cvb, cvc = bank()
                        A('pe', lambda e, off=off, c=c, n=n, cvb=cvb: e.transpose(cvb[0:30, (c % 4) * 128:(c % 4 + 1) * 128], arf(off + n * 4, 30), con(C_ID)),
                          reads=arc(off + n * 4, 120) + ["CON"], writes=[cvc])
                        if c % 4 == 3:
                            lo = 46 * KB + (c // 4) * 2048
                            A('act', lambda e, cvb=cvb, lo=lo: e.activation(out=arf(lo, 512)[0:30, :], in_=cvb[0:30, :], func=AF.Copy), reads=[cvc], writes=arc(lo, 2048))
                            A('sp', lambda e, lo=lo, q=q, c=c: e.dma_start(out=CONVOUT[q][:, (c // 4) * 512:(c // 4 + 1) * 512], in_=arf(lo, 512)[0:30, :]), reads=arc(lo, 2048), dma=True)
                    elif sg['kind'] == 'prompt':
                        A('pool', lambda e, off=off, c=c, n=n: e.tensor_copy(out=UH[:, c, :], in_=arf(off + n * 4, 30)), reads=arc(off + n * 4, 120), writes=[("UH", c)])
                    zoff = Z_OFF + c * 2048 + c0 * 4
                    A('dve', lambda e, off=off, zoff=zoff, c=c, n=n: e.tensor_scalar(out=arf(zoff, n), in0=arf(off, n), scalar1=PAR[:, P_CW + c * 31:P_CW + c * 31 + 1],
                                                                                      scalar2=PAR[:, P_CB + c:P_CB + c + 1], op0=ALU.mult, op1=ALU.add),
                      reads=arc(off, n * 4) + ["PAR"], writes=arc(zoff, n * 4))
            for tap in range(1, CW):
                for si, sg in enumerate(segs):
                    n, c0 = sg['n'], sg['col0']
                    for c in range(8):
                        off = useg(si, c)
                        zoff = Z_OFF + c * 2048 + c0 * 4
                        A('dve', lambda e, off=off, zoff=zoff, c=c, n=n, tap=tap: e.scalar_tensor_tensor(
                            out=arf(zoff, n), in0=arf(off + tap * 4, n), scalar=PAR[:, P_CW + c * 31 + tap:P_CW + c * 31 + tap + 1], in1=arf(zoff, n),
                            op0=ALU.mult, op1=ALU.add),
                          reads=arc(off + tap * 4, n * 4) + arc(zoff, n * 4), writes=arc(zoff, n * 4))
            pm, pmc = bank()
            pq, pqc = bank()
            for c in range(8):
                zoff = Z_OFF + c * 2048
                A('pe', lambda e, zoff=zoff, c=c: e.matmul(pm[:, 0:T], lhsT=con(C_ONESM), rhs=arf(zoff, T), start=(c == 0), stop=(c == 7)),
                  reads=["CON"] + arc(zoff, T * 4), writes=[pmc], fp32pe=True)
            for c in range(8):
                zoff = Z_OFF + c * 2048
                A('act', lambda e, zoff=zoff: e.activation(out=SQ[:, 0:T], in_=arf(zoff, T), func=AF.Square), reads=arc(zoff, T * 4), writes=["SQ"])
                A('pe', lambda e, c=c: e.matmul(pq[:, 0:T], lhsT=con(C_ONESM), rhs=SQ[:, 0:T], start=(c == 0), stop=(c == 7)),
                  reads=["CON", "SQ"], writes=[pqc], fp32pe=True)
            MEAN, RSTD = LN_OFF, LN_OFF + 2048
            A('act', lambda e: e.activation(out=arf(MEAN, T), in_=pm[:, 0:T], func=AF.Copy), reads=[pmc], writes=arc(MEAN, T * 4))
            A('dve', lambda e: e.tensor_tensor(out=arf(RSTD, T), in0=arf(MEAN, T), in1=arf(MEAN, T), op=ALU.mult), reads=arc(MEAN, T * 4), writes=arc(RSTD, T * 4))
            A('dve', lambda e: e.tensor_tensor(out=arf(RSTD, T), in0=pq[:, 0:T], in1=arf(RSTD, T), op=ALU.subtract), reads=[pqc] + arc(RSTD, T * 4), writes=arc(RSTD, T * 4))
            A('act', lambda e: e.activation(out=arf(RSTD, T), in_=arf(RSTD, T), func=AF.Ln, bias=EPS, scale=1.0), reads=arc(RSTD, T * 4), writes=arc(RSTD, T * 4))
            A('act', lambda e: e.activation(out=arf(RSTD, T), in_=arf(RSTD, T), func=AF.Exp, scale=-0.5), reads=arc(RSTD, T * 4), writes=arc(RSTD, T * 4))
            for c in range(8):
                zoff = Z_OFF + c * 2048
                zb = ZB_OFF + c * 1024
                t1 = T1[c % 2]
                t2 = T2[c % 2]
                A('dve', lambda e, zoff=zoff, t1=t1: e.tensor_tensor(out=t1[:, 0:T], in0=arf(zoff, T), in1=arf(MEAN, T), op=ALU.subtract),
                  reads=arc(zoff, T * 4) + arc(MEAN, T * 4), writes=[("T1", c % 2)])
                A('pool', lambda e, t1=t1: e.tensor_tensor(out=t1[:, 0:T], in0=t1[:, 0:T], in1=arf(RSTD, T), op=ALU.mult),
                  reads=[("T1", c % 2)] + arc(RSTD, T * 4), writes=[("T1", c % 2)])
                A('pool', lambda e, t1=t1, c=c: e.tensor_scalar(out=t1[:, 0:T], in0=t1[:, 0:T], scalar1=PAR[:, P_LG + c:P_LG + c + 1], scalar2=PAR[:, P_LB + c:P_LB + c + 1],
                                                                op0=ALU.mult, op1=ALU.add), reads=[("T1", c % 2), "PAR"], writes=[("T1", c % 2)])
                A('act', lambda e, t1=t1, t2=t2: e.activation(out=t2[:, 0:T], in_=t1[:, 0:T], func=AF.Sigmoid), reads=[("T1", c % 2)], writes=[("T2", c % 2)])
                A('dve', lambda e, t1=t1, t2=t2, zb=zb: e.tensor_tensor(out=arb(zb, T), in0=t1[:, 0:T], in1=t2[:, 0:T], op=ALU.mult),
                  reads=[("T1", c % 2), ("T2", c % 2)], writes=arc(zb, T * 2))

        def cumsum_chunk(rows, lf_ap, lf_cells, tri_off, carries, c_out, c_cells, nck_out, nck_cells, car_upd):
            pb, pc = bank()
            n_mm = 1 + len(carries)
            A('pe', lambda e: e.matmul(pb[0:rows, 0:8], lhsT=CON[0:rows, tri_off:tri_off + rows], rhs=lf_ap, start=True, stop=(n_mm == 1)),
              reads=["CON"] + lf_cells, writes=[pc], fp32pe=True)
            for i, (sel_off, car_ap, car_cells) in enumerate(carries):
                A('pe', lambda e, sel_off=sel_off, car_ap=car_ap, i=i: e.matmul(pb[0:rows, 0:8], lhsT=CON[:, sel_off:sel_off + rows], rhs=car_ap, start=False, stop=(i == n_mm - 2)),
                  reads=["CON"] + car_cells, writes=[pc], fp32pe=True)
            A('act', lambda e: e.activation(out=c_out, in_=pb[0:rows, 0:8], func=AF.Copy), reads=[pc], writes=c_cells)
            A('dve', lambda e: e.tensor_scalar(out=nck_out, in0=pb[0:rows, 0:8], scalar1=-1.0, scalar2=None, op0=ALU.mult), reads=[pc], writes=nck_cells)
            if car_upd is not None:
                car_ap, car_cells = car_upd
                pb2, pc2 = bank()
                A('pe', lambda e: e.matmul(pb2[:, 0:8], lhsT=CON[0:rows, C_ONES:C_ONES + 128], rhs=lf_ap, start=True, stop=False),
                  reads=["CON"] + lf_cells, writes=[pc2], fp32pe=True)
                A('pe', lambda e: e.matmul(pb2[:, 0:8], lhsT=con(C_E0), rhs=car_ap, start=False, stop=True), reads=["CON"] + car_cells, writes=[pc2], fp32pe=True)
                A('act', lambda e: e.activation(out=car_ap, in_=pb2[:, 0:8], func=AF.Copy), reads=[pc2], writes=car_cells)

        def kt_ingest(rows, kb_ap_fn, kb_cells, qs, kb):
            pt, ptc = ptbank()
            for h in range(NH):
                A('pe', lambda e, h=h: e.transpose(pt[:, h * 128:h * 128 + rows], kb_ap_fn(h), IDB[0:rows, 0:rows]), reads=kb_cells + ["IDB"], writes=[ptc])
            kst = KST_OFF
            A('act', lambda e: e.activation(out=arb(kst, 1024).rearrange("p (h t) -> p h t", h=NH)[:, :, 0:rows],
                                            in_=pt[:].rearrange("p (h t) -> p h t", h=NH)[:, :, 0:rows], func=AF.Copy),
              reads=[ptc], writes=arc(kst, 2048))
            for q in qs:
                A('sp', lambda e, q=q: e.dma_start(out=KTd[q].rearrange("h d t -> d h t")[:, :, kb * 128:kb * 128 + rows],
                                                   in_=arb(kst, 1024).rearrange("p (h t) -> p h t", h=NH)[:, :, 0:rows]),
                  reads=arc(kst, 2048), writes=[("KTd", q, kb)], dma=True)

        def phase_qkv(tile):
            T = tile['T']
            chunks = tile['chunks']
            nch = len(chunks)
            stq = lambda ci: ST_OFF + ci * 2048
            stk = lambda ci: ST_OFF + 8 * KB + ci * 2048
            stv = lambda ci: ST_OFF + 16 * KB + ci * 2048
            for ct in range(6):
                kind = ct // 2
                half = ct % 2
                buf = wload_cols("in", 16, O_Q + ct * 512, 512)
                Wt, wc = buf
                if dbg is not None:
                    A('sp', lambda e, Wt=Wt, ct=ct: e.dma_start(out=dbg[ct], in_=Wt[:, :]), reads=[wc], dma=True)
                for ci, ch in enumerate(chunks):
                    rows = ch['rows']
                    pb, pc = bank()
                    for k in range(16):
                        A('pe', lambda e, pb=pb, k=k, ci=ci, rows=rows: e.matmul(pb[0:rows, :], lhsT=HT[:, k, ci * 128:ci * 128 + rows], rhs=Wt[:, k * 512:(k + 1) * 512],
                                                                                    start=(k == 0), stop=(k == 15)),
                          reads=[wc, ("HT", k)], writes=[pc])
                    if kind < 2:
                        gcol = P_QG if kind == 0 else P_KG
                        A('act', lambda e, pb=pb, rows=rows: e.activation(out=SQ[0:rows, :], in_=pb[0:rows, :], func=AF.Square), reads=[pc], writes=["SQ"])
                        A('dve', lambda e, rows=rows: e.tensor_reduce(out=SM[0:rows, 0:4], in_=SQ[0:rows, :].rearrange("p (a b) -> p a b", a=4), axis=AX.X, op=ALU.add),
                          reads=["SQ"], writes=["SM0"])
                        A('act', lambda e, rows=rows: e.activation(out=SM[0:rows, 0:4], in_=SM[0:rows, 0:4], func=AF.Ln, bias=EPS, scale=1.0 / HD), reads=["SM0"], writes=["SM0"])
                        A('act', lambda e, rows=rows: e.activation(out=SM[0:rows, 0:4], in_=SM[0:rows, 0:4], func=AF.Exp, scale=-0.5), reads=["SM0"], writes=["SM0"])
                        kf = KF[(ct * nch + ci) % 2]
                        kfc = ("KF", (ct * nch + ci) % 2)
                        A('dve', lambda e, pb=pb, rows=rows, kf=kf: e.tensor_tensor(out=kf[0:rows, :].rearrange("p (a b) -> p a b", a=4), in0=pb[0:rows, :].rearrange("p (a b) -> p a b", a=4),
                                                                                 in1=SM[0:rows, 0:4].unsqueeze(2).to_broadcast([rows, 4, 128]), op=ALU.mult),
                          reads=[pc, "SM0"], writes=[kfc])
                        A('pool', lambda e, rows=rows, kf=kf, gcol=gcol: e.tensor_tensor(out=kf[0:rows, :].rearrange("p (a b) -> p a b", a=4), in0=kf[0:rows, :].rearrange("p (a b) -> p a b", a=4),
                                                                                      in1=PAR[0:rows, gcol:gcol + 128].unsqueeze(1).to_broadcast([rows, 4, 128]), op=ALU.mult),
                          reads=[kfc, "PAR"], writes=[kfc])
                        so = (stq(ci) if kind == 0 else stk(ci)) + half * 1024
                        A('pool', lambda e, rows=rows, kf=kf, so=so: e.tensor_copy(out=arb(so, 512)[0:rows, :], in_=kf[0:rows, :]), reads=[kfc], writes=arc(so, 1024))
                        if kind == 1:
                            for (p0, n, q, t0) in ch['rg']:
                                A('sp', lambda e, kf=kf, p0=p0, n=n, q=q, t0=t0, half=half: e.dma_start(out=KOUT[q][t0:t0 + n, half * 512:(half + 1) * 512], in_=kf[p0:p0 + n, :]),
                                  reads=[kfc], dma=True)
                    else:
                        kf = KF[(ct * nch + ci) % 2]
                        kfc = ("KF", (ct * nch + ci) % 2)
                        A('act', lambda e, pb=pb, rows=rows, kf=kf: e.activation(out=kf[0:rows, :], in_=pb[0:rows, :], func=AF.Copy), reads=[pc], writes=[kfc])
                        so = stv(ci) + half * 1024
                        A('pool', lambda e, rows=rows, kf=kf, so=so: e.tensor_copy(out=arb(so, 512)[0:rows, :], in_=kf[0:rows, :]), reads=[kfc], writes=arc(so, 1024))
                        for (p0, n, q, t0) in ch['rg']:
                            A('sp', lambda e, kf=kf, p0=p0, n=n, q=q, t0=t0, half=half: e.dma_start(out=VOUT[q][t0:t0 + n, half * 512:(half + 1) * 512], in_=kf[p0:p0 + n, :]),
                              reads=[kfc], dma=True)
            buf = wload_cols("in", 16, O_F, 8)
            Wt, wc = buf
            for ci, ch in enumerate(chunks):
                rows = ch['rows']
                pb, pc = bank()
                for k in range(16):
                    A('pe', lambda e, pb=pb, k=k, ci=ci, rows=rows: e.matmul(pb[0:rows, 0:8], lhsT=HT[:, k, ci * 128:ci * 128 + rows], rhs=Wt[:, k * 8:(k + 1) * 8],
                                                                                start=(k == 0), stop=(k == 15)),
                      reads=[wc, ("HT", k)], writes=[pc])
                t = SM[0:rows, 8:16]
                a = SM[0:rows, 16:24]
                m = SM[0:rows, 24:32]
                A('dve', lambda e, pb=pb, rows=rows, t=t: e.tensor_tensor(out=t, in0=pb[0:rows, 0:8], in1=PAR[0:rows, P_BF:P_BF + 8], op=ALU.add), reads=[pc, "PAR"], writes=["SM1"])
                A('dve', lambda e, t=t, a=a: e.tensor_scalar(out=a, in0=t, scalar1=-1.0, scalar2=None, op0=ALU.mult), reads=["SM1"], writes=["SM2"])
                A('dve', lambda e, t=t, a=a: e.tensor_tensor(out=a, in0=a, in1=t, op=ALU.max), reads=["SM1", "SM2"], writes=["SM2"])
                A('act', lambda e, a=a: e.activation(out=a, in_=a, func=AF.Exp, scale=-1.0), reads=["SM2"], writes=["SM2"])
                A('act', lambda e, a=a: e.activation(out=a, in_=a, func=AF.Ln, bias=1.0, scale=1.0), reads=["SM2"], writes=["SM2"])
                A('dve', lambda e, t=t, m=m: e.tensor_scalar_min(out=m, in0=t, scalar1=0.0), reads=["SM1"], writes=["SM3"])
                A('dve', lambda e, a=a, m=m, ci=ci, rows=rows: e.tensor_tensor(out=LF[0:rows, ci, :], in0=m, in1=a, op=ALU.subtract), reads=["SM2", "SM3"], writes=[("LF", ci)])
                for (p0, n, q, t0) in ch['rg']:
                    A('sp', lambda e, ci=ci, p0=p0, n=n, q=q, t0=t0: e.dma_start(out=LFOUT[q][t0:t0 + n, :], in_=LF[p0:p0 + n, ci, :]), reads=[("LF", ci)], dma=True)
            for ci, ch in enumerate(chunks):
                rows = ch['rows']
                pt, ptc = ptbank()
                for h in range(NH):
                    so = stq(ci) + h * 256
                    A('pe', lambda e, h=h, so=so, rows=rows, pt=pt: e.transpose(pt[:, h * 128:h * 128 + rows], arb(so, 128)[0:rows, :], IDB[0:rows, 0:rows]),
                      reads=arc(so, 256) + ["IDB"], writes=[ptc])
                A('dve', lambda e, ci=ci, rows=rows, pt=pt: e.tensor_copy(out=arb(QT_OFF, 4096).rearrange("p (h t) -> p h t", h=NH)[:, :, ci * 128:ci * 128 + rows],
                                                                        in_=pt[:].rearrange("p (h t) -> p h t", h=NH)[:, :, 0:rows]),
                  reads=[ptc], writes=arc(QT_OFF, 8192))
                kb = ch['kb']
                qs = ch['qs']
                kt_ingest(rows, lambda h, ci=ci, rows=rows: arb(stk(ci) + h * 256, 128)[0:rows, :], arc(stk(ci), 2048), qs, kb)
                for q in qs:
                    A('sp', lambda e, q=q, ci=ci, rows=rows, kb=kb: e.dma_start(out=Vd[q].rearrange("h p k d -> p h k d")[0:rows, :, kb, :],
                                                                              in_=arb(stv(ci), 1024)[0:rows, :].rearrange("p (h d) -> p h d", h=NH)),
                      reads=arc(stv(ci), 2048), writes=[("Vd", q, kb)], dma=True)
                if tile['kind'] == 'prompt':
                    q = qs[0]
                    cumsum_chunk(rows, LF[0:rows, ci, :], [("LF", ci)], C_TRI, [(C_E0, CAR[:, q, :], [("CAR", q)])],
                                 CC[0:rows, ci, :], [("CC", ci)], NCK[0:rows, q, kb, :], [("NCK", q, kb)], (CAR[:, q, :], [("CAR", q)]))
                else:
                    cumsum_chunk(rows, LF[0:rows, ci, :], [("LF", ci)], C_TRIS,
                                 [(C_EA, CAR[:, 2, :], [("CAR", 2)]), (C_EB, CAR[:, 3, :], [("CAR", 3)])],
                                 CC[0:rows, ci, :], [("CC", ci)], NCK[0:rows, 2, kb, :], [("NCK", 2, kb)], None)
            for ci, ch in enumerate(chunks):
                rows = ch['rows']
                for hf in range(2):
                    dgt = T1[hf]
                    dgc = ("T1", hf)
                    A('dve', lambda e, dgt=dgt, rows=rows, ci=ci, hf=hf: e.tensor_tensor(
                        out=dgt[0:rows, 0:4 * rows].rearrange("p (h t) -> p h t", h=4),
                        in0=CON[0:rows, C_ID:C_ID + rows].unsqueeze(1).to_broadcast([rows, 4, rows]),
                        in1=CC[0:rows, ci, hf * 4:hf * 4 + 4].unsqueeze(2).to_broadcast([rows, 4, rows]), op=ALU.mult),
                      reads=["CON", ("CC", ci)], writes=[dgc])
                    pb, pc = bank()
                    A('pe', lambda e, dgt=dgt, rows=rows, pb=pb: e.matmul(pb[:, 0:4 * rows], lhsT=CON[0:rows, C_ONES:C_ONES + 128], rhs=dgt[0:rows, 0:4 * rows],
                                                                         start=True, stop=True), reads=["CON", dgc], writes=[pc], fp32pe=True)
                    cq0 = CQ_OFF + hf * 4 * 2048 + ci * 128 * 4
                    A('act', lambda e, pb=pb, cq0=cq0, rows=rows: e.activation(
                        out=arf(cq0, 4 * 512).rearrange("p (h t) -> p h t", h=4)[:, :, 0:rows],
                        in_=pb[:, 0:4 * rows].rearrange("p (h t) -> p h t", h=4), func=AF.Copy),
                      reads=[pc], writes=arc(CQ_OFF + hf * 4 * 2048, 4 * 2048))

        def phase_attn(tile):
            T = tile['T']
            scale = HD ** -0.5
            it = [0]
            for si, sg in enumerate(tile['segs']):
                q, c0, n = sg['q'], sg['col0'], sg['n']
                nkb = sg['nkb']
                for h in range(NH):
                    kvo = KV_OFF + (it[0] % 2) * 8 * KB
                    it[0] += 1
                    kth = kvo
                    vh = kvo + 4096
                    A('sp', lambda e, q=q, h=h, kth=kth, nkb=nkb: e.dma_start(out=arb(kth, nkb * 128), in_=KTd[q, h][:, 0:nkb * 128]),
                      reads=[("KTd", q, kb) for kb in range(nkb)], writes=arc(kth, nkb * 256), dma=True)
                    A('sp', lambda e, q=q, h=h, vh=vh, nkb=nkb: e.dma_start(out=arb(vh, nkb * 128).rearrange("p (k d) -> p k d", k=nkb), in_=Vd[q, h][:, 0:nkb, :]),
                      reads=[("Vd", q, kb) for kb in range(nkb)], writes=arc(vh, nkb * 256), dma=True)
                    po, poc = bank()
                    pd, pdc = bank()
                    cq = CQ_OFF + h * 2048
                    DEPTH = 3
                    info = {}
                    for step in range(nkb + DEPTH):
                        if step < nkb:
                            kb = step
                            M, mask_ap, nckq = sg['blk'](kb)
                            pss, psc = bank((poc, pdc))
                            A('pe', lambda e, pss=pss, kth=kth, kb=kb, M=M, h=h, c0=c0, n=n: e.matmul(pss[0:M, 0:n], lhsT=arb(kth + kb * 256, M), rhs=arb(QT_OFF + h * 1024 + c0 * 2, n),
                                                                                                       start=True, stop=True),
                              reads=arc(kth + kb * 256, M * 2) + arc(QT_OFF + h * 1024, 1024), writes=[psc])
                            tt = TTV[kb % 4]
                            ttc = TTC[kb % 4]
                            A('dve', lambda e, pss=pss, tt=tt, M=M, cq=cq, c0=c0, n=n: e.scalar_tensor_tensor(out=tt[0:M, 0:n], in0=pss[0:M, 0:n], scalar=scale, in1=arf(cq + c0 * 4, n)[0:M, :],
                                                                                                               op0=ALU.mult, op1=ALU.add),
                              reads=[psc] + arc(cq, T * 4), writes=ttc)
                            if mask_ap is not None:
                                A('dve', lambda e, tt=tt, M=M, n=n, mask_ap=mask_ap: e.tensor_tensor(out=tt[0:M, 0:n], in0=tt[0:M, 0:n], in1=mask_ap, op=ALU.add),
                                  reads=ttc + ["CON"], writes=ttc)
                            pp = PP[kb % 4]
                            ppc = ("PP", kb % 4)
                            A('act', lambda e, tt=tt, pp=pp, M=M, n=n, nckq=nckq, kb=kb, h=h: e.activation(out=pp[0:M, 0:n], in_=tt[0:M, 0:n], func=AF.Exp,
                                                                                                            bias=NCK[0:M, nckq, kb, h:h + 1], scale=1.0),
                              reads=ttc + [("NCK", nckq, kb)], writes=[ppc])
                            info[kb] = (M, pp, ppc)
                        if step >= DEPTH:
                            kb = step - DEPTH
                            M, pp, ppc = info[kb]
                            A('pe', lambda e, po=po, vh=vh, kb=kb, M=M, pp=pp, n=n, nkb=nkb: e.matmul(po[:, 0:n], lhsT=arb(vh + kb * 256, 128)[0:M, :], rhs=pp[0:M, 0:n],
                                                                                                       start=(kb == 0), stop=(kb == nkb - 1)),
                              reads=arc(vh + kb * 256, 256) + [ppc], writes=[poc])
                            A('pe', lambda e, pd=pd, kb=kb, M=M, pp=pp, n=n, nkb=nkb: e.matmul(pd[:, 0:n], lhsT=ONB[0:M, :], rhs=pp[0:M, 0:n],
                                                                                               start=(kb == 0), stop=(kb == nkb - 1)),
                              reads=["ONB", ppc], writes=[pdc])
                    A('dve', lambda e, pd=pd, n=n: e.reciprocal(out=RD[:, 0:n], in_=pd[:, 0:n]), reads=[pdc], writes=["RD"])
                    ao = ATT_OFF + h * 1024 + c0 * 2
                    A('dve', lambda e, po=po, n=n, ao=ao: e.tensor_tensor(out=arb(ao, n), in0=po[:, 0:n], in1=RD[:, 0:n], op=ALU.mult),
                      reads=[poc, "RD"], writes=arc(ATT_OFF + h * 1024, 1024))

        def phase_mix(tile):
            T = tile['T']
            for g2 in range(8):
                bc = wbuf()
                wload_cols("pw", 8, g2 * 256, 256, bc, 0)
                wload_cols("ao", 8, g2 * 256, 256, bc, 2048)
                Wt, wc = bc
                for g in range(2):
                    fo = g2 * 2 + g
                    res = []
                    for (boff, rhs, rc) in [
                        (0, lambda k: arb(ZB_OFF + k * 1024, T), lambda k: arc(ZB_OFF + k * 1024, T * 2)),
                        (2048, lambda k: arb(ATT_OFF + k * 1024, T), lambda k: arc(ATT_OFF + k * 1024, T * 2)),
                    ]:
                        pb, pc = bank()
                        for k in range(8):
                            A('pe', lambda e, pb=pb, Wt=Wt, k=k, g=g, boff=boff, rhs=rhs: e.matmul(pb[:, 0:T], lhsT=Wt[:, boff + k * 256 + g * 128: boff + k * 256 + (g + 1) * 128],
                                                                                                rhs=rhs(k), start=(k == 0), stop=(k == 7)),
                              reads=[wc] + rc(k), writes=[pc])
                        res.append((pb, pc))
                    (pa, pac), (pbo, pboc) = res
                    t1, t2 = T1[fo % 2], T2[fo % 2]
                    ga, gac = gate_ap(0, fo, T)
                    gb, gbc = gate_ap(1, fo, T)
                    A('dve', lambda e, t1=t1, pa=pa, ga=ga: e.tensor_tensor(out=t1[:, 0:T], in0=pa[:, 0:T], in1=ga, op=ALU.mult), reads=[pac] + gac, writes=[("T1", fo % 2)])
                    A('dve', lambda e, t2=t2, pbo=pbo, gb=gb: e.tensor_tensor(out=t2[:, 0:T], in0=pbo[:, 0:T], in1=gb, op=ALU.mult), reads=[pboc] + gbc, writes=[("T2", fo % 2)])
                    mo = MIX_OFF + fo * 1024
                    A('pool', lambda e, t1=t1, t2=t2, mo=mo: e.tensor_tensor(out=arb(mo, T), in0=t1[:, 0:T], in1=t2[:, 0:T], op=ALU.add),
                      reads=[("T1", fo % 2), ("T2", fo % 2)], writes=arc(mo, T * 2))

        def phase_wout(tile):
            for nt in range(4):
                buf = wload_cols("o", 16, nt * 512, 512)
                Wt, wc = buf
                for ci, ch in enumerate(tile['chunks']):
                    rows = ch['rows']
                    pb, pc = bank()
                    for k in range(16):
                        A('pe', lambda e, pb=pb, k=k, ci=ci, rows=rows: e.matmul(pb[0:rows, :], lhsT=arb(MIX_OFF + k * 1024 + ci * 256, rows), rhs=Wt[:, k * 512:(k + 1) * 512],
                                                                                    start=(k == 0), stop=(k == 15)),
                          reads=[wc] + arc(MIX_OFF + k * 1024, 1024), writes=[pc])
                    A('dve', lambda e, pb=pb, ci=ci, rows=rows, nt=nt: e.tensor_tensor(out=X[0:rows, ci, nt * 512:(nt + 1) * 512], in0=pb[0:rows, :], in1=X[0:rows, ci, nt * 512:(nt + 1) * 512],
                                                                                         op=ALU.add), reads=[pc, ("X", ci)], writes=[("X", ci)])

        def phase_ffn(tile):
            T = tile['T']
            segs = tile['segs']
            for f2 in range(NF // 2):
                bgu = wbuf()
                wload_cols("up", 16, f2 * 256, 256, bgu, 0)
                wload_cols("up", 16, DFF + f2 * 256, 256, bgu, 4096)
                Wt, wc = bgu
                for g in range(2):
                    f = f2 * 2 + g
                    res = []
                    for boff in (0, 4096):
                        pb, pc = bank()
                        for k in range(16):
                            A('pe', lambda e, pb=pb, Wt=Wt, k=k, g=g, boff=boff: e.matmul(pb[:, 0:T], lhsT=Wt[:, boff + k * 256 + g * 128:boff + k * 256 + (g + 1) * 128], rhs=HT[:, k, 0:T],
                                                                                  start=(k == 0), stop=(k == 15)),
                              reads=[wc, ("HT", k)], writes=[pc])
                        res.append((pb, pc))
                    (pg, pgc), (pu, puc) = res
                    for si, sg in enumerate(segs):
                        q, c0, n = sg['q'], sg['col0'], sg['n']
                        go = G_OFF + (f % 2) * 4160 + si * 2080
                        ft = FT_OFF + (f % 2) * 2048
                        A('pool', lambda e, go=go, q=q, f=f: e.tensor_copy(out=arf(go, 2), in_=GH[:, q, f, :]), reads=[("GH", q)], writes=arc(go, 8))
                        A('act', lambda e, go=go, pg=pg, c0=c0, n=n: e.activation(out=arf(go + 8, n), in_=pg[:, c0:c0 + n], func=AF.Copy), reads=[pgc], writes=arc(go + 8, n * 4))
                        A('pool', lambda e, go=go, q=q, f=f, n=n: e.tensor_copy(out=GH[:, q, f, :], in_=arf(go + n * 4, 2)), reads=arc(go + n * 4, 8), writes=[("GH", q)])
                        A('dve', lambda e, go=go, ft=ft, f=f, n=n: e.tensor_scalar(out=arf(ft, n), in0=arf(go, n), scalar1=PAR[:, P_FW + f * 3:P_FW + f * 3 + 1],
                                                                                  scalar2=PAR[:, P_FB + f:P_FB + f + 1], op0=ALU.mult, op1=ALU.add),
                          reads=arc(go, n * 4) + ["PAR"], writes=arc(ft, n * 4))
                        for tap in (1, 2):
                            A('dve', lambda e, go=go, ft=ft, f=f, n=n, tap=tap: e.scalar_tensor_tensor(out=arf(ft, n), in0=arf(go + tap * 4, n),
                                                                                                     scalar=PAR[:, P_FW + f * 3 + tap:P_FW + f * 3 + tap + 1], in1=arf(ft, n),
                                                                                                     op0=ALU.mult, op1=ALU.add),
                              reads=arc(go + tap * 4, n * 4) + arc(ft, n * 4) + ["PAR"], writes=arc(ft, n * 4))
                        A('act', lambda e, ft=ft, n=n: e.activation(out=arf(ft, n), in_=arf(ft, n), func=AF.Silu), reads=arc(ft, n * 4), writes=arc(ft, n * 4))
                        ao = ACT_OFF + f * 1024 + c0 * 2
                        A('dve', lambda e, ft=ft, n=n, pu=pu, c0=c0, ao=ao: e.tensor_tensor(out=arb(ao, n), in0=pu[:, c0:c0 + n], in1=arf(ft, n), op=ALU.mult),
                          reads=[puc] + arc(ft, n * 4), writes=arc(ACT_OFF + f * 1024, 1024))
            for si, sg in enumerate(segs):
                if sg['last']:
                    q = sg['q']

                    for f4 in range(NF // 4):
                        fb, fc = bank()
                        for g in range(4):
                            f = f4 * 4 + g
                            A('pe', lambda e, fb=fb, g=g, f=f, q=q: e.transpose(fb[0:2, g * 128:(g + 1) * 128], GH[:, q, f, :], con(C_ID)),
                              reads=[("GH", q), "CON"], writes=[fc])
                        A('act', lambda e, fb=fb: e.activation(out=SQ[0:2, :], in_=fb[0:2, :], func=AF.Copy), reads=[fc], writes=["SQ"])
                        A('sp', lambda e, q=q, f4=f4: e.dma_start(out=FFNOUT[q][:, f4 * 512:(f4 + 1) * 512], in_=SQ[0:2, :]), reads=["SQ"], dma=True)
            chunks = tile['chunks']
            pieces = [(0, 16), (16, 16), (32, 12)]
            for nt in range(4):
                banks = [bank() for _ in chunks]
                for pi, (f0, nf) in enumerate(pieces):
                    buf = wbuf()
                    Wt, wc = buf
                    src = wb["dn"].rearrange("(f p) n -> p f n", p=128)[:, f0:f0 + nf, nt * 512:(nt + 1) * 512]
                    dst = Wt[:, 0:nf * 512].rearrange("p (f n) -> p f n", f=nf)
                    A('sp', lambda e, src=src, dst=dst: e.dma_start(out=dst, in_=src), reads=wcells("dn"), writes=[wc], dma=True)
                    for ci, ch in enumerate(chunks):
                        rows = ch['rows']
                        pb, pc = banks[ci]
                        for j in range(nf):
                            f = f0 + j
                            A('pe', lambda e, pb=pb, Wt=Wt, j=j, f=f, ci=ci, rows=rows: e.matmul(pb[0:rows, :], lhsT=arb(ACT_OFF + f * 1024 + ci * 256, rows), rhs=Wt[:, j * 512:(j + 1) * 512],
                                                                                                  start=(f == 0), stop=(f == NF - 1)),
                              reads=[wc] + arc(ACT_OFF + f * 1024, 1024), writes=[pc])
                for ci, ch in enumerate(chunks):
                    rows = ch['rows']
                    pb, pc = banks[ci]
                    A('dve', lambda e, pb=pb, ci=ci, rows=rows, nt=nt: e.tensor_tensor(out=X[0:rows, ci, nt * 512:(nt + 1) * 512], in0=pb[0:rows, :], in1=X[0:rows, ci, nt * 512:(nt + 1) * 512],
                                                                                         op=ALU.add), reads=[pc, ("X", ci)], writes=[("X", ci)])
            for ci, ch in enumerate(chunks):
                for (p0, n, q, t0) in ch['rg']:
                    A('sp', lambda e, ci=ci, p0=p0, n=n, q=q, t0=t0: e.dma_start(out=YOUT[q][t0:t0 + n, :], in_=X[p0:p0 + n, ci, :]), reads=[("X", ci)], dma=True)

        def sample_past(q):
            b = q - 2
            L0, L1, SK, SV = 44 * KB, 48 * KB, 52 * KB, 54 * KB
            for kb in range(8):
                A('sp', lambda e, kb=kb: e.dma_start(out=arf(L0, 1024), in_=ck[b, kb * 128:(kb + 1) * 128, :]), writes=arc(L0, 4096), dma=True)
                A('dve', lambda e: e.tensor_copy(out=arb(SK, 1024), in_=arf(L0, 1024)), reads=arc(L0, 4096), writes=arc(SK, 2048))
                kt_ingest(128, lambda h: arb(SK + h * 256, 128), arc(SK, 2048), [q], kb)
                A('sp', lambda e, kb=kb: e.dma_start(out=arf(L1, 1024), in_=cv[b, kb * 128:(kb + 1) * 128, :]), writes=arc(L1, 4096), dma=True)
                A('dve', lambda e: e.tensor_copy(out=arb(SV, 1024), in_=arf(L1, 1024)), reads=arc(L1, 4096), writes=arc(SV, 2048))
                A('sp', lambda e, kb=kb: e.dma_start(out=Vd[q].rearrange("h p k d -> p h k d")[:, :, kb, :],
                                                     in_=arb(SV, 1024).rearrange("p (h d) -> p h d", h=NH)),
                  reads=arc(SV, 2048), writes=[("Vd", q, kb)], dma=True)
                ci = kb % 4
                A('sp', lambda e, kb=kb, ci=ci: e.dma_start(out=LF[:, ci, :], in_=clf[b, kb * 128:(kb + 1) * 128, :]), writes=[("LF", ci)], dma=True)
                cumsum_chunk(128, LF[:, ci, :], [("LF", ci)], C_TRI, [(C_E0, CAR[:, q, :], [("CAR", q)])],
                             CC[:, ci, :], [("CC", ci)], NCK[:, q, kb, :], [("NCK", q, kb)], (CAR[:, q, :], [("CAR", q)]))

        def mask_prompt(i, n):
            o = C_MASK + 384 - 128 * i
            return CON[:, o:o + n]

        tiles = []
        for s in range(2):
            for j in range(4):
                def blk(kb, j=j, s=s):
                    if kb < 4 * j:
                        return 128, None, s
                    return 128, mask_prompt(kb - 4 * j, 512), s
                tiles.append(dict(kind='prompt', T=512,
                                  chunks=[dict(rows=128, rg=[(0, 128, s, j * 512 + c * 128)], kb=j * 4 + c, qs=[s]) for c in range(4)],
                                  segs=[dict(kind='prompt', q=s, col0=0, n=512, first=(j == 0), last=(j == 3), nkb=4 * j + 4, blk=blk)]))

        def blk_s(kb, q):
            if kb < 8:
                return 128, None, q
            c0 = 0 if q == 2 else 32
            return 48, CON[0:48, C_MASKS + c0:C_MASKS + c0 + 16], 2
        tiles.append(dict(kind='sample', T=64,
                          chunks=[dict(rows=64, rg=[(0, 16, 2, 0), (32, 16, 3, 0)], kb=8, qs=[2, 3])],
                          segs=[dict(kind='sample', q=2, col0=0, n=16, first=False, last=True, nkb=9, blk=lambda kb: blk_s(kb, 2)),
                                dict(kind='sample', q=3, col0=32, n=16, first=False, last=True, nkb=9, blk=lambda kb: blk_s(kb, 3))]))

        if tile_sel is not None:
            tiles = [tiles[i] for i in tile_sel]
        early_past = any(t['kind'] == 'sample' for t in tiles) and os.environ.get('DBG_EARLY_PAST', '1') == '1'
        if early_past:
            for q in (2, 3):
                A('sp', lambda e, q=q: e.dma_start(out=arf(44 * KB, 1024)[0:30, :], in_=stc[q - 2]), writes=arc(44 * KB, 4096), dma=True)
                pb, pc = bank()
                for c in range(8):
                    A('pe', lambda e, pb=pb, c=c: e.transpose(pb[:, c * 32:c * 32 + 30], arf(44 * KB + c * 512, 128)[0:30, :], CON[0:30, C_ID:C_ID + 30]),
                      reads=arc(44 * KB, 4096) + ["CON"], writes=[pc])
                A('act', lambda e, pb=pb, q=q: e.activation(out=UHS[:, q - 2, :, :], in_=pb[:, 0:256].rearrange("p (c t) -> p c t", c=8)[:, :, 0:30], func=AF.Copy),
                  reads=[pc], writes=[("UHS", q - 2, c) for c in range(8)])
                A('sp', lambda e, q=q: e.dma_start(out=arf(0, DFF)[0:2, :], in_=stf[q - 2]), writes=arc(0, DFF * 4), dma=True)
                pb, pc = bank()
                for f in range(NF):
                    A('pe', lambda e, pb=pb, f=f: e.transpose(pb[:, f * 2:f * 2 + 2], arf(f * 512, 128)[0:2, :], CON[0:2, C_ID:C_ID + 2]),
                      reads=arc(f * 512, 512) + ["CON"], writes=[pc])
                A('act', lambda e, pb=pb, q=q: e.activation(out=GH[:, q, :, :].rearrange("p f t -> p (f t)"), in_=pb[:, 0:2 * NF], func=AF.Copy),
                  reads=[pc], writes=[("GH", q)])
            for q in (2, 3):
                sample_past(q)
        for nm in ["in", "pw", "ao", "o", "up", "dn"]:
            wconv(nm)
        for tile in tiles:
            if os.environ.get('DBG_BARRIER', '1') == '1':
                S.barrier()
            if tile['kind'] == 'sample':
                A('pool', lambda e: e.memset(X[:, 0, :], 0.0), writes=[("X", 0)])
                assert early_past
                A('pool', lambda e: e.memset(ARf[:], 0.0), writes=arc(0, ARENA_KB * KB))
            load_x(tile)
            if upto >= 1:
                norm_transpose(tile, P_G1)
                if os.environ.get('DBG_PB', '1') == '1':
                    S.barrier()
            if upto >= 2:
                phase_glu_conv(tile)
                phase_gate_proj(tile)
                phase_conv_ln(tile)
                if os.environ.get('DBG_PB', '1') == '1':
                    S.barrier()
            if upto >= 3:
                phase_qkv(tile)
                if os.environ.get('DBG_PB', '1') == '1':
                    S.barrier()
            if upto >= 4:
                if tile['kind'] == 'sample':
                    A('pool', lambda e: e.memset(arb(ATT_OFF, 4096), 0.0), writes=arc(ATT_OFF, 8192))
                phase_attn(tile)
                if os.environ.get('DBG_PB', '1') == '1':
                    S.barrier()
            if upto >= 5:
                phase_mix(tile)
                if os.environ.get('DBG_PB', '1') == '1':
                    S.barrier()
            if upto >= 6:
                if tile['kind'] == 'sample':
                    A('pool', lambda e: e.memset(X[:, 0, :], 0.0), writes=[("X", 0)])
                load_x(tile)
                phase_wout(tile)
                if os.environ.get('DBG_PB', '1') == '1':
                    S.barrier()
            if upto == 6:
                for ci, ch in enumerate(tile['chunks']):
                    for (p0, n, q, t0) in ch['rg']:
                        A('sp', lambda e, ci=ci, p0=p0, n=n, q=q, t0=t0: e.dma_start(out=YOUT[q][t0:t0 + n, :], in_=X[p0:p0 + n, ci, :]), reads=[("X", ci)], dma=True)
            if upto >= 7:
                norm_transpose(tile, P_G2)
                if os.environ.get('DBG_PB', '1') == '1':
                    S.barrier()
            if upto >= 8:
                if tile['kind'] == 'sample':
                    A('pool', lambda e: e.memset(arb(ACT_OFF, 44 * 512), 0.0), writes=arc(ACT_OFF, 44 * 1024))
                phase_ffn(tile)

        if dbg2 is not None:
            A('sp', lambda e: e.dma_start(out=dbg2[:, 0:4 * KBMAX * 8], in_=NCK[:].rearrange("p a b c -> p (a b c)")), reads=[("NCK", q, kb) for q in range(4) for kb in range(KBMAX)], dma=True)
            A('sp', lambda e: e.dma_start(out=dbg2[:, 4 * KBMAX * 8:], in_=CAR[:].rearrange("p a c -> p (a c)")), reads=[("CAR", q) for q in range(4)], dma=True)
        S.emit(nc, block, csem, dsem)
    return nc


_NC = [None]


def kernel(x_prompt, x_sample, cache_k, cache_v, cache_logf, state_conv, state_ffn,
           norm_mix_g, w_in, b_forget, conv_dw_w, conv_dw_b, conv_ln_g, conv_ln_b,
           conv_pw_out, q_norm_g, k_norm_g, w_att_out, w_out,
           norm_ffn_g, w_up, ffn_dw_w, ffn_dw_b, w_down):
    f = lambda a: np.ascontiguousarray(np.asarray(a, dtype=np.float32))
    x_prompt, x_sample = f(x_prompt), f(x_sample)
    ckk = f(cache_k)[0].reshape(16, PAST, AW)
    cvv = f(cache_v)[0].reshape(16, PAST, AW)
    clff = f(cache_logf)[0]
    stc = f(state_conv)[0]
    stf = f(state_ffn)[0]
    params = make_params(f(norm_mix_g)[0], f(b_forget)[0], f(conv_dw_w)[0], f(conv_dw_b)[0], f(conv_ln_g)[0], f(conv_ln_b)[0],
                         f(q_norm_g)[0], f(k_norm_g)[0], f(norm_ffn_g)[0], f(ffn_dw_w)[0], f(ffn_dw_b)[0])
    consts = make_consts()
    W = {"w_in": f(w_in)[0], "w_pw": f(conv_pw_out)[0], "w_ao": f(w_att_out)[0], "w_o": f(w_out)[0], "w_up": f(w_up)[0], "w_dn": f(w_down)[0]}
    if _NC[0] is None:
        _NC[0] = build_program()
    nc = _NC[0]
    in_maps = []
    for c in range(8):
        m = {"xp": x_prompt[2 * c:2 * c + 2], "xs": x_sample[2 * c:2 * c + 2], "ck": ckk[2 * c:2 * c + 2], "cv": cvv[2 * c:2 * c + 2],
             "clf": clff[2 * c:2 * c + 2], "stc": stc[2 * c:2 * c + 2], "stf": stf[2 * c:2 * c + 2], "params": params, "consts": consts}
        m.update(W)
        in_maps.append(m)
    res = run_bass_kernel_spmd(nc, in_maps, core_ids=list(range(8)))
    R = res.results
    cat = lambda k: np.concatenate([np.asarray(r[k]) for r in R], axis=0)
    y_p = cat("yp")
    y_s = cat("ys")
    p_k = cat("pk").reshape(1, 16, SEQ, NH, HD)
    p_v = cat("pv").reshape(1, 16, SEQ, NH, HD)
    p_lf = cat("plf").reshape(1, 16, SEQ, NH)
    p_c = cat("pconv").reshape(1, 16, CW - 1, DC)
    p_f = cat("pffn").reshape(1, 16, 2, DFF)
    s_k = cat("sk").reshape(1, 16, DSEQ, NH, HD)
    s_v = cat("sv").reshape(1, 16, DSEQ, NH, HD)
    s_lf = cat("slf").reshape(1, 16, DSEQ, NH)
    s_c = cat("sconv").reshape(1, 16, CW - 1, DC)
    s_f = cat("sffn").reshape(1, 16, 2, DFF)
    return (y_p, y_s, p_k, p_v, p_lf, p_c, p_f, s_k, s_v, s_lf, s_c, s_f)
```
